# Optimizing a Trainium2 kernel written in Bass

```python
import math
import jax
import jax.numpy as jnp
from jax import lax
import numpy as np

D_MODEL = 1024
BATCH = 16
SEQ = 256
DEPTH = 2
DEC_BATCH = 8
DEC_SEQ = 2048
PAST_LEN = 256

GRID_W = 64
N_MIXERS = 2
N_CONV_LAYERS = (DEPTH + 1) // 2
N_SSD_LAYERS = DEPTH // 2
CONV_WIDTH = 2 * D_MODEL
CONV_K = 31
SSD_INNER = 2 * D_MODEL
SSD_HEAD_DIM = 64
SSD_HEADS = SSD_INNER // SSD_HEAD_DIM
SSD_GROUPS = 4
D_STATE = 128
SSD_CONV_K = 5
CHUNK = 128
XBC_DIM = SSD_INNER + 2 * SSD_GROUPS * D_STATE
SSD_PROJ_DIM = SSD_INNER + XBC_DIM + 2 * SSD_HEADS
EPS = 1e-6

kernel_name = 'conv_ssd_hybrid_diffusion_step'


def rms_norm(x, g):
    xf = x.astype(jnp.float32)
    y = xf * lax.rsqrt(jnp.mean(xf * xf, axis=-1, keepdims=True) + EPS)
    return (y * g.astype(jnp.float32)).astype(x.dtype)


def layer_norm(x, g, b):
    xf = x.astype(jnp.float32)
    mu = jnp.mean(xf, axis=-1, keepdims=True)
    xc = xf - mu
    var = jnp.mean(xc * xc, axis=-1, keepdims=True)
    y = xc * lax.rsqrt(var + EPS) * g.astype(jnp.float32) + b.astype(jnp.float32)
    return y.astype(x.dtype)


def depthwise_conv(x, w, b):
    k = w.shape[0]
    y = lax.conv_general_dilated(x, w[:, None, :], window_strides=(1,), padding=[(k // 2, k // 2)],
                                 dimension_numbers=('NWC', 'WIO', 'NWC'), feature_group_count=x.shape[-1])
    return y + b


def adaln_params(cond, w, b):
    m = jax.nn.silu(cond) @ w + b
    return jnp.split(m, 3, axis=-1)


def conformer_branch(h, on_grid, w_in, b_in, w_dw, b_dw, ln_g, ln_b, w_out):
    u = h @ w_in + b_in
    a, a_gate, z = jnp.split(u, 3, axis=-1)
    v = a * jax.nn.sigmoid(a_gate)
    nb, length, e = v.shape
    if on_grid:
        rows = length // GRID_W
        v = v.reshape(nb * rows, GRID_W, e)
    v = depthwise_conv(v, w_dw, b_dw).reshape(nb, length, e)
    v = jax.nn.silu(layer_norm(v, ln_g, ln_b))
    return (v * jax.nn.silu(z)) @ w_out


def ssd_scan(x, dt, a_neg, bm, cm, h0):
    f32 = jnp.float32
    b, l, nh, p = x.shape
    g, n = bm.shape[2], bm.shape[3]
    r = nh // g
    c = l // CHUNK
    xdt = (x.astype(f32) * dt[..., None]).reshape(b, c, CHUNK, g, r, p)
    la = jnp.moveaxis((dt * a_neg).reshape(b, c, CHUNK, g, r), 2, -1)
    a_cum = jnp.cumsum(la, axis=-1)
    bc = bm.astype(f32).reshape(b, c, CHUNK, g, n)
    cc = cm.astype(f32).reshape(b, c, CHUNK, g, n)
    lower = jnp.tril(jnp.ones((CHUNK, CHUNK), dtype=bool))
    seg = a_cum[..., :, None] - a_cum[..., None, :]
    decay_mat = jnp.exp(jnp.where(lower, seg, -jnp.inf))
    cb = jnp.einsum('bclgn,bcsgn->bcgls', cc, bc)
    y_diag = jnp.einsum('bcgrls,bcsgrp->bclgrp', cb[:, :, :, None] * decay_mat, xdt)
    decay_to_end = jnp.exp(a_cum[..., -1:] - a_cum)
    chunk_states = jnp.einsum('bcsgn,bcgrs,bcsgrp->bcgrpn', bc, decay_to_end, xdt)
    chunk_decay = jnp.exp(a_cum[..., -1])

    def step(state, inp):
        dec, st = inp
        return state * dec[..., None, None] + st, state

    h0g = h0.astype(f32).reshape(b, g, r, p, n)
    final, prev = lax.scan(step, h0g, (jnp.moveaxis(chunk_decay, 1, 0), jnp.moveaxis(chunk_states, 1, 0)))
    prev = jnp.moveaxis(prev, 0, 1)
    y_off = jnp.einsum('bclgn,bcgrpn,bcgrl->bclgrp', cc, prev, jnp.exp(a_cum))
    y = (y_diag + y_off).reshape(b, l, nh, p)
    return y, final.reshape(b, nh, p, n)


def ssd_branch(h, h0_f, h0_b, w_in, w_conv, b_conv, dt_bias_f, dt_bias_b, a_log_f, a_log_b, d_skip, norm_g, w_out):
    f32 = jnp.float32
    b, l, _ = h.shape
    u = h @ w_in
    z, xbc, dt_raw = jnp.split(u, [SSD_INNER, SSD_INNER + XBC_DIM], axis=-1)
    xbc = jax.nn.silu(depthwise_conv(xbc, w_conv, b_conv))
    xs, bm, cm = jnp.split(xbc, [SSD_INNER, SSD_INNER + SSD_GROUPS * D_STATE], axis=-1)
    xs = xs.reshape(b, l, SSD_HEADS, SSD_HEAD_DIM)
    bm = bm.reshape(b, l, SSD_GROUPS, D_STATE)
    cm = cm.reshape(b, l, SSD_GROUPS, D_STATE)
    dtr_f, dtr_b = jnp.split(dt_raw.astype(f32), 2, axis=-1)
    dt_f = jax.nn.softplus(dtr_f + dt_bias_f.astype(f32))
    dt_b = jax.nn.softplus(dtr_b + dt_bias_b.astype(f32))
    a_f = -jnp.exp(a_log_f.astype(f32))
    a_b = -jnp.exp(a_log_b.astype(f32))
    y_f, s_f = ssd_scan(xs, dt_f, a_f, bm, cm, h0_f)
    flip = lambda t: jnp.flip(t, axis=1)
    y_b, s_b = ssd_scan(flip(xs), flip(dt_b), a_b, flip(bm), flip(cm), h0_b)
    y = y_f + flip(y_b) + d_skip.astype(f32)[:, None] * xs.astype(f32)
    y = y.reshape(b, l, SSD_INNER) * jax.nn.silu(z.astype(f32))
    y = rms_norm(y.reshape(b, l, SSD_GROUPS, SSD_INNER // SSD_GROUPS),
                 norm_g.reshape(SSD_GROUPS, SSD_INNER // SSD_GROUPS)).reshape(b, l, SSD_INNER)
    return y.astype(h.dtype) @ w_out, s_f, s_b


def setup_inputs(seed: int = 0) -> dict:
    key = jax.random.key(seed)
    ks = jax.random.split(key, 32)
    f32 = jnp.float32
    nc, ns = N_CONV_LAYERS, N_SSD_LAYERS

    def nrm(k, shape, s):
        return jax.random.normal(k, shape, f32) * s

    def dt_bias(k):
        dt = jnp.exp(jax.random.uniform(k, (ns, SSD_HEADS), f32, math.log(1e-3), math.log(1e-1)))
        return dt + jnp.log(-jnp.expm1(-dt))

    state_shape = (DEC_BATCH, ns, SSD_HEADS, SSD_HEAD_DIM, D_STATE)
    return {
        'x_prompt': nrm(ks[0], (BATCH, SEQ, D_MODEL), 1.0),
        'x_sample': nrm(ks[1], (DEC_BATCH, DEC_SEQ, D_MODEL), 1.0),
        'state_fwd': nrm(ks[2], state_shape, 0.5),
        'state_bwd': nrm(ks[3], state_shape, 0.5),
        'c': nrm(ks[4], (DEC_BATCH, D_MODEL), 1.0),
        'c_ctx': nrm(ks[5], (D_MODEL,), 1.0),
        'ada_w': nrm(ks[6], (DEPTH, D_MODEL, 3 * D_MODEL), 0.5 * D_MODEL ** -0.5),
        'ada_b': nrm(ks[7], (DEPTH, 3 * D_MODEL), 0.02),
        'norm_g': 1.0 + nrm(ks[8], (DEPTH, D_MODEL), 0.1),
        'conv_w_in': nrm(ks[9], (nc, D_MODEL, 3 * CONV_WIDTH), D_MODEL ** -0.5),
        'conv_b_in': nrm(ks[10], (nc, 3 * CONV_WIDTH), 0.02),
        'conv_w_dw': nrm(ks[11], (nc, CONV_K, CONV_WIDTH), CONV_K ** -0.5),
        'conv_b_dw': nrm(ks[12], (nc, CONV_WIDTH), 0.02),
        'conv_ln_g': 1.0 + nrm(ks[13], (nc, CONV_WIDTH), 0.1),
        'conv_ln_b': nrm(ks[14], (nc, CONV_WIDTH), 0.02),
        'conv_w_out': nrm(ks[15], (nc, CONV_WIDTH, D_MODEL), CONV_WIDTH ** -0.5),
        'ssd_w_in': nrm(ks[16], (ns, D_MODEL, SSD_PROJ_DIM), D_MODEL ** -0.5),
        'ssd_w_conv': nrm(ks[17], (ns, SSD_CONV_K, XBC_DIM), SSD_CONV_K ** -0.5),
        'ssd_b_conv': nrm(ks[18], (ns, XBC_DIM), 0.02),
        'ssd_dt_bias_f': dt_bias(ks[19]),
        'ssd_dt_bias_b': dt_bias(ks[20]),
        'ssd_a_log_f': jnp.log(jax.random.uniform(ks[21], (ns, SSD_HEADS), f32, 1.0, 16.0)),
        'ssd_a_log_b': jnp.log(jax.random.uniform(ks[22], (ns, SSD_HEADS), f32, 1.0, 16.0)),
        'ssd_d': 1.0 + nrm(ks[23], (ns, SSD_HEADS), 0.1),
        'ssd_norm_g': 1.0 + nrm(ks[24], (ns, SSD_INNER), 0.1),
        'ssd_w_out': nrm(ks[25], (ns, SSD_INNER, D_MODEL), SSD_INNER ** -0.5),
        'final_norm_g': 1.0 + nrm(ks[26], (D_MODEL,), 0.1),
    }


def reference(x_prompt, x_sample, state_fwd, state_bwd, c, c_ctx, ada_w, ada_b, norm_g,
              conv_w_in, conv_b_in, conv_w_dw, conv_b_dw, conv_ln_g, conv_ln_b, conv_w_out,
              ssd_w_in, ssd_w_conv, ssd_b_conv, ssd_dt_bias_f, ssd_dt_bias_b, ssd_a_log_f, ssd_a_log_b,
              ssd_d, ssd_norm_g, ssd_w_out, final_norm_g):
    yp = x_prompt
    ys = x_sample
    cond_ctx = c_ctx[None, None, :]
    cond_lat = c[:, None, :]
    new_f = []
    new_b = []
    for i in range(DEPTH):
        j = i // N_MIXERS
        sh_p, sc_p, gt_p = adaln_params(cond_ctx, ada_w[i], ada_b[i])
        sh_s, sc_s, gt_s = adaln_params(cond_lat, ada_w[i], ada_b[i])
        hp = rms_norm(yp, norm_g[i]) * (1.0 + sc_p) + sh_p
        hs = rms_norm(ys, norm_g[i]) * (1.0 + sc_s) + sh_s
        if i % N_MIXERS == 0:
            cw = (conv_w_in[j], conv_b_in[j], conv_w_dw[j], conv_b_dw[j], conv_ln_g[j], conv_ln_b[j], conv_w_out[j])
            out_p = conformer_branch(hp, False, *cw)
            out_s = conformer_branch(hs, True, *cw)
        else:
            sw = (ssd_w_in[j], ssd_w_conv[j], ssd_b_conv[j], ssd_dt_bias_f[j], ssd_dt_bias_b[j],
                  ssd_a_log_f[j], ssd_a_log_b[j], ssd_d[j], ssd_norm_g[j], ssd_w_out[j])
            zero_state = jnp.zeros((yp.shape[0], SSD_HEADS, SSD_HEAD_DIM, D_STATE), jnp.float32)
            out_p, s_f, s_b = ssd_branch(hp, zero_state, zero_state, *sw)
            out_s, _, _ = ssd_branch(hs, state_fwd[:, j], state_bwd[:, j], *sw)
            new_f.append(s_f)
            new_b.append(s_b)
        yp = yp + gt_p * out_p
        ys = ys + gt_s * out_s
    y_prompt = rms_norm(yp, final_norm_g)
    y_sample = rms_norm(ys, final_norm_g)
    new_state_fwd = jnp.stack(new_f, axis=1).astype(x_prompt.dtype)
    new_state_bwd = jnp.stack(new_b, axis=1).astype(x_prompt.dtype)
    return (y_prompt, y_sample, new_state_fwd, new_state_bwd)
```

```python
import contextlib
import os
import numpy as np
import concourse.bass as bass
import concourse.mybir as mybir
from concourse.bass_utils import run_bass_kernel_spmd

F32 = mybir.dt.float32
BF16 = mybir.dt.bfloat16
AF = mybir.ActivationFunctionType
ALU = mybir.AluOpType

NCORES = 8
D = 1024
E = 2048
NTOK = 2560
EPS = 1e-6


class Res:
    __slots__ = ("w", "r")

    def __init__(self):
        self.w = {}
        self.r = {}


class KB:
    def __init__(self, nc, es):
        self.nc = nc
        self.es = es
        self.eng = {"pe": nc.tensor, "act": nc.scalar, "dve": nc.vector,
                    "pool": nc.gpsimd, "sp": nc.sync}
        self.semh = {}
        self.cnt = {}
        for e in ("pe", "act", "dve", "pool"):
            self.semh[e] = es.enter_context(nc.semaphore("s_" + e))
            self.cnt[e] = 0
        self.seen = {e: {} for e in self.eng}
        self.nwait = 0
        self.nins = 0
        self.stacks = [es]

    def push(self):
        st = contextlib.ExitStack()
        self.stacks.append(st)

    def pop(self):
        self.stacks.pop().close()

    def barrier(self):
        for e in self.eng:
            self._wait(e, {s: v for s, v in self.cnt.items() if v > 0 and s != e})

    def sb(self, name, shape, dt=F32):
        return self.stacks[-1].enter_context(self.nc.sbuf_tensor("sb_" + name, list(shape), dt))

    def ps(self, name, shape, dt=F32):
        return self.es.enter_context(self.nc.psum_tensor("ps_" + name, list(shape), dt))

    def dsem(self, name):
        self.semh[name] = self.es.enter_context(self.nc.semaphore("d_" + name))
        self.cnt[name] = 0
        return name

    def _wait(self, e, deps):
        for s, v in deps.items():
            if self.seen[e].get(s, 0) < v:
                self.eng[e].wait_ge(self.semh[s], v)
                self.seen[e][s] = v
                self.nwait += 1

    @staticmethod
    def _deps(e, reads, writes):
        deps = {}
        for r in reads:
            for s, v in r.w.items():
                if e == "pe" and s == "pe":
                    continue
                if deps.get(s, 0) < v:
                    deps[s] = v
        for w in writes:
            for s, v in w.w.items():
                if s == e:
                    continue
                if deps.get(s, 0) < v:
                    deps[s] = v
            for s, v in w.r.items():
                if s == e:
                    continue
                if deps.get(s, 0) < v:
                    deps[s] = v
        return deps

    def op(self, e, fn, reads=(), writes=()):
        self._wait(e, self._deps(e, reads, writes))
        ins = fn(self.eng[e])
        self.cnt[e] += 1
        c = self.cnt[e]
        ins.then_inc(self.semh[e], 1)
        self.nins += 1
        for r in reads:
            r.r[e] = c
        for w in writes:
            w.w[e] = c
            w.r = {}
        return ins

    def dma(self, q, sem, out, in_, reads=(), writes=(), **kw):
        self._wait(q, self._deps(q, reads, writes))
        self.cnt[sem] += 16
        c = self.cnt[sem]
        self.eng[q].dma_start(out=out, in_=in_, **kw).then_inc(self.semh[sem], 16)
        self.nins += 1
        for r in reads:
            r.r[sem] = c
        for w in writes:
            w.w[sem] = c
            w.r = {}

    def finish(self, sems):
        for s in sems:
            if self.cnt[s] > 0:
                self.nc.sync.wait_ge(self.semh[s], self.cnt[s])


PK = {}
_off = 0
for _n, _w in (("cond", 16), ("ada_b", 48), ("norm_g", 16), ("cb_in", 48), ("cw_dw", 496),
               ("cb_dw", 16), ("cln_g", 16), ("cln_b", 16), ("sw_conv", 120), ("sb_conv", 24),
               ("dtb", 64), ("alog", 64), ("sd", 32), ("ident", 128)):
    PK[_n] = (_off, _off + _w)
    _off += _w
PK_N = _off


class _Stop(Exception):
    pass


def build_program(stop_after_l0=False):
    nc = bass.Bass("TRN2", target_bir_lowering=False)
    stage_n = int(os.environ.get("K_STAGE", "0"))
    try:
        _emit(nc, stop_after_l0, stage_n)
    except _Stop:
        pass
    return nc


def _emit(nc, stop_after_l0, STAGE):

    def din(name, shape):
        return nc.dram_tensor(name, list(shape), F32, kind="ExternalInput").ap()

    x_d = din("x", [NTOK, D])
    stf_d = din("stf", [2048, 128])
    stb_d = din("stb", [2048, 128])
    pk_d = din("pk", [128, PK_N])
    adabg_d = din("ada_bg", [128, 2048])
    sng_d = din("sng", [128, 2048])
    fng_d = din("fng", [128, 1024])
    adaw_d = din("ada_w", [2, 1024, 3072])
    cwin_d = din("cw_in", [1024, 6144])
    cwout_d = din("cw_out", [2048, 1024])
    swin_d = din("sw_in", [1024, 5184])
    swout_d = din("sw_out", [2048, 1024])
    y_d = nc.dram_tensor("y", [NTOK, D], F32, kind="ExternalOutput").ap()
    nsf_d = nc.dram_tensor("nsf", [2, 2048, 128], F32, kind="ExternalOutput").ap()
    nsb_d = nc.dram_tensor("nsb", [2, 2048, 128], F32, kind="ExternalOutput").ap()
    y1_d = nc.dram_tensor("y1", [NTOK, D], F32, kind="Internal").ap()
    dbg_d = nc.dram_tensor("dbg", [12, 128, 1024], F32, kind="ExternalOutput").ap() if STAGE else None
    sbs_d = nc.dram_tensor("sbs", [20, 128, 2048], BF16, kind="Internal").ap()
    dgs_d = nc.dram_tensor("dgs", [16, 128, 31 * 128], BF16, kind="Internal").ap()
    xsp_d = nc.dram_tensor("xsp", [10, 128, 24 * 256], BF16, kind="Internal").ap()
    zsp_d = nc.dram_tensor("zsp", [10, 2, 128, 2048], BF16, kind="Internal").ap()
    dsp_d = nc.dram_tensor("dsp", [10, 128, 128], F32, kind="Internal").ap()

    with contextlib.ExitStack() as es:
        k = KB(nc, es)
        store_sems = []

        if STAGE:
            store_sems.append(k.dsem("dbg"))

        def dump(i, ap, w, reads):
            if STAGE:
                k.dma("pool", "dbg", dbg_d[i, :, 0:w], ap, reads=reads)

        def stage(n):
            if STAGE == n:
                k.barrier()
                while len(k.stacks) > 1:
                    k.pop()
                k.finish(store_sems)
                print("STAGE stop", n, "instructions", k.nins, "waits", k.nwait)
                raise _Stop()

        pk = k.sb("pk", [128, PK_N])
        r_pk = Res()
        k.dsem("pk")
        k.dma("sp", "pk", pk[:], pk_d[:, :], writes=[r_pk])

        def pkc(name, a=0, b=None):
            lo, hi = PK[name]
            return pk[:, lo + a: (lo + b) if b is not None else hi]

        identf = pkc("ident")
        identb = k.sb("identb", [128, 128], BF16)
        r_idb = Res()
        k.op("dve", lambda e: e.tensor_copy(out=identb[:], in_=identf), reads=[r_pk], writes=[r_idb])
        onesE = k.sb("onesE", [128, 128], BF16)
        r_onesE = Res()
        k.op("pool", lambda e: e.memset(onesE[:], 1.0 / E), writes=[r_onesE])

        banks = [(k.ps("bank%d" % i, [128, 512]), Res()) for i in range(8)]
        rot = {"i": 0}

        rotcfg = {"base": 2, "n": 6}

        def bank():
            b = banks[rotcfg["base"] + rot["i"] % rotcfg["n"]]
            rot["i"] += 1
            return b

        NS = 3
        wflat = k.sb("wflat", [128, 3, 8, 512], BF16)
        wr = [wflat[:, i] for i in range(NS)]
        wflat2 = wflat[:].rearrange("p s k c -> p (s k c)").rearrange("p (s k c) -> p s k c", s=2, k=8)
        wr2 = [wflat2[:, i] for i in range(2)]
        r_wr = [Res() for _ in range(NS)]
        r_wr2 = [Res() for _ in range(2)]
        for i in range(NS):
            k.dsem("wr%d" % i)
        wstate = {"i": 0}

        def wslot():
            i = wstate["i"] % NS
            wstate["i"] += 1
            return i

        Gt = k.sb("Gt", [128, 8, 2])
        SHt = k.sb("SHt", [128, 8, 2])
        r_G = Res()
        GT = k.sb("GT", [128, 2, 1024], BF16)
        r_GT = Res()
        k.dsem("adabg")

        def adaln(l):
            k.push()
            r_adabg = Res()
            r_msb = Res()
            r_scb = Res()
            r_screp = Res()
            adabg = k.sb("adabg%d" % l, [128, 1024])
            msb = k.sb("msb%d" % l, [128, 16, 2])
            scb = k.sb("scb%d" % l, [128, 8, 2], BF16)
            screp = k.sb("screp%d" % l, [128, 2, 8, 128], BF16)
            k.op("act", lambda e: e.activation(out=scb[:].rearrange("p k c -> p (k c)"), in_=pkc("cond"), func=AF.Silu),
                 reads=[r_pk], writes=[r_scb])
            for c in range(2):
                k.op("dve", lambda e, c=c: e.tensor_copy(out=screp[:, c, :, :],
                                                         in_=scb[:, :, c:c + 1].to_broadcast([128, 8, 128])),
                     reads=[r_scb], writes=[r_screp])
            k.dma("sp", "adabg", adabg[:], adabg_d[:, l * 1024:(l + 1) * 1024], writes=[r_adabg])
            pm, r_pm = banks[0]
            for q in range(6):
                s = wslot()
                k.dma("pool", "wr%d" % s, wr[s],
                      adaw_d[l, :, q * 512:(q + 1) * 512].rearrange("(k p) c -> p k c", p=128), writes=[r_wr[s]])
                if q < 4:
                    for jj in range(4):
                        j = q * 4 + jj
                        for kk in range(8):
                            k.op("pe", lambda e, s=s, jj=jj, j=j, kk=kk: e.matmul(
                                pm[:, j * 2:(j + 1) * 2], lhsT=wr[s][:, kk, jj * 128:(jj + 1) * 128],
                                rhs=scb[:, kk, :], start=(kk == 0), stop=(kk == 7)),
                                reads=[r_wr[s], r_scb], writes=[r_pm])
                else:
                    for c in range(2):
                        pg, r_pg = bank()
                        for kk in range(8):
                            k.op("pe", lambda e, s=s, c=c, kk=kk, pg=pg: e.matmul(
                                pg[:], lhsT=screp[:, c, kk, :], rhs=wr[s][:, kk, :],
                                start=(kk == 0), stop=(kk == 7)),
                                reads=[r_wr[s], r_screp], writes=[r_pg])
                        k.op("dve", lambda e, c=c, q=q, pg=pg: e.tensor_tensor(
                            out=GT[:, c, (q - 4) * 512:(q - 3) * 512], in0=pg[:],
                            in1=adabg[:, (q - 4) * 512:(q - 3) * 512], op=ALU.add),
                            reads=[r_pg, r_adabg], writes=[r_GT])
            lo = PK["ada_b"][0] + l * 24
            k.op("dve", lambda e: e.tensor_tensor(
                out=msb[:], in0=pm[:, 0:32].rearrange("p (j c) -> p j c", c=2),
                in1=pk[:, lo:lo + 16].unsqueeze(2).to_broadcast([128, 16, 2]), op=ALU.add),
                reads=[r_pm, r_pk], writes=[r_msb])
            k.op("dve", lambda e: e.tensor_copy(out=SHt[:], in_=msb[:, 0:8, :]), reads=[r_msb], writes=[r_G])
            glo = PK["norm_g"][0] + l * 8
            k.op("dve", lambda e: e.scalar_tensor_tensor(
                out=Gt[:], in0=msb[:, 8:16, :], scalar=1.0,
                in1=pk[:, glo:glo + 8].unsqueeze(2).to_broadcast([128, 8, 2]), op0=ALU.add, op1=ALU.mult),
                reads=[r_msb, r_pk], writes=[r_G])
            k.barrier()
            k.pop()

        junk = k.sb("junk", [128, 1024], BF16)
        r_junk = Res()

        def sumsq(x_ap, r_x, npart, ss_col, r_ss):
            k.op("act", lambda e: e.activation(out=junk[0:npart, :], in_=x_ap, func=AF.Square, accum_out=ss_col),
                 reads=[r_x], writes=[r_junk, r_ss])

        def rstd_of(ss_ap, rs_ap, npart, r_ss, scale):
            k.op("act", lambda e: e.activation(out=rs_ap, in_=ss_ap, func=AF.Ln, scale=scale, bias=epsc[0:npart, :]),
                 reads=[r_ss, r_eps], writes=[r_ss])
            k.op("act", lambda e: e.activation(out=rs_ap, in_=rs_ap, func=AF.Exp, scale=-0.5),
                 reads=[r_ss], writes=[r_ss])

        def xn_transpose(x_ap, r_x, npart, cond, hT_out_fn, r_hT, rs_col, r_ss, phase=None):
            if phase in (None, "a"):
                k.op("dve", lambda e: e.tensor_scalar(out=x_ap, in0=x_ap, scalar1=rs_col,
                                                      scalar2=0.0, op0=ALU.mult, op1=ALU.add),
                     reads=[r_x, r_ss], writes=[r_x])
            if phase == "a":
                return
            for half in range(2):
                pt, r_pt = bank()
                for kk in range(4):
                    kq = half * 4 + kk
                    k.op("pe", lambda e, kk=kk, kq=kq, pt=pt: e.transpose(
                        out=pt[:, kk * 128:kk * 128 + npart], in_=x_ap[:, kq * 128:(kq + 1) * 128],
                        identity=identf[0:npart, 0:npart]),
                        reads=[r_x, r_pk], writes=[r_pt])
                for kk in range(4):
                    kq = half * 4 + kk
                    k.op("act", lambda e, kk=kk, kq=kq, pt=pt: e.activation(
                        out=hT_out_fn(kq), in_=pt[:, kk * 128:kk * 128 + npart], func=AF.Identity,
                        scale=Gt[:, kq, cond:cond + 1], bias=SHt[:, kq, cond:cond + 1]),
                        reads=[r_pt, r_G], writes=[r_hT])

        epsc = k.sb("epsc", [128, 1])
        r_eps = Res()
        k.op("pool", lambda e: e.memset(epsc[:], EPS), writes=[r_eps])

        adaln(0)
        dump(10, GT[:, 0, :], 1024, [r_GT])
        dump(11, Gt[:].rearrange("p k c -> p (k c)"), 16, [r_G])
        stage(1)
        wout = k.sb("wout", [128, 16, 1024], BF16)
        r_wout = Res()
        k.dsem("wout")

        def load_wout(src):
            for q in range(4):
                k.dma("pool", "wout", wout[:, q * 4:(q + 1) * 4, :],
                      src[q * 512:(q + 1) * 512, :].rearrange("(e p) c -> p e c", p=128), writes=[r_wout])

        load_wout(cwout_d)

        k.push()
        w05 = k.sb("w05", [128, 16, 31])
        r_w05 = Res()
        k.op("dve", lambda e: e.tensor_scalar(out=w05[:].rearrange("p a b -> p (a b)"), in0=pkc("cw_dw"),
                                              scalar1=0.5, scalar2=0.0, op0=ALU.mult, op1=ALU.add),
             reads=[r_pk], writes=[r_w05])
        bg05 = k.sb("bg05", [128, 16])
        k.op("dve", lambda e: e.tensor_scalar(out=bg05[:], in0=pkc("cb_in", 16, 32), scalar1=0.5, scalar2=0.0,
                                              op0=ALU.mult, op1=ALU.add),
             reads=[r_pk], writes=[r_w05])

        xt = [k.sb("xt%d" % i, [128, 4, 1024]) for i in range(1)] * 2
        r_xt = [Res()] * 2
        k.dsem("xt0")
        xr = [k.sb("xr%d" % i, [128, 1024]) for i in range(2)]
        r_xr = [Res() for _ in range(2)]
        for i in range(2):
            k.dsem("xr%d" % i)
        ss0 = [k.sb("ss0_%d" % i, [128, 8]) for i in range(2)]
        rs0 = [k.sb("rs0_%d" % i, [128, 8]) for i in range(2)]
        r_ss0 = [Res() for _ in range(2)]
        hT = [k.sb("hT%d" % i, [128, 8, 512], BF16) for i in range(1)] * 2
        r_hT = [Res()] * 2
        a_t = [k.sb("a_t%d" % i, [128, 512]) for i in range(2)]
        th_t = [k.sb("th_t%d" % i, [128, 512]) for i in range(2)]
        r_a = [Res() for _ in range(2)]
        r_th = [Res() for _ in range(2)]
        vp = [k.sb("vp%d" % i, [128, 768], BF16) for i in range(2)]
        r_vp = [Res() for _ in range(2)]
        dg = [k.sb("dg%d" % i, [128, 31, 128], BF16) for i in range(2)]
        r_dg = [Res() for _ in range(2)]
        sqb = [k.sb("sqb%d" % i, [128, 512], BF16) for i in range(2)]
        r_sqb = [Res() for _ in range(2)]
        cv = k.sb("cv", [128, 16, 512], BF16)
        r_cv = [Res() for _ in range(16)]
        sz = k.sb("sz", [128, 16, 512], BF16)
        r_sz = [Res() for _ in range(16)]
        mean_t = k.sb("mean_t", [128, 512])
        rstd_t = k.sb("rstd_t", [128, 512])
        r_mean = Res()
        r_rstd = Res()
        t1 = [k.sb("t1_%d" % i, [128, 512]) for i in range(2)]
        r_t1 = [Res() for _ in range(2)]
        ub = [k.sb("ub%d" % i, [128, 512], BF16) for i in range(2)]
        r_ub = [Res() for _ in range(2)]
        yt = [k.sb("yt%d" % i, [128, 1024]) for i in range(2)]
        r_yt = [Res() for _ in range(2)]
        for i in range(2):
            store_sems.append(k.dsem("yt%d" % i))
        ytst = {"i": 0}

        NG0 = 5
        print("L0 sbuf bytes remaining", nc.sbuf_bytes_remaining)

        def l0_geom(g):
            return (8, 64, 0) if g < 4 else (2, 256, 1)

        def l0_xload(g):
            b = g % 2
            k.dma("sp", "xt0", xt[b][:], x_d[g * 512:(g + 1) * 512, :].rearrange("(t p) d -> p t d", p=128),
                  writes=[r_xt[b]])

        def l0_front(g):
            b = g % 2
            cond = l0_geom(g)[2]
            for tt in range(4):
                sumsq(xt[b][:, tt, :], r_xt[b], 128, ss0[b][:, tt:tt + 1], r_ss0[b])
            rstd_of(ss0[b][:, 0:4], rs0[b][:, 0:4], 128, r_ss0[b], 1.0 / D)
            for tt in range(4):
                xn_transpose(xt[b][:, tt, :], r_xt[b], 128, cond,
                             lambda kq, tt=tt, b=b: hT[b][:, kq, tt * 128:(tt + 1) * 128], r_hT[b],
                             rs0[b][:, tt:tt + 1], r_ss0[b])

        wr2x = k.sb("wr2x", [128, 8, 768], BF16)
        wr2.append(wr2x[:])
        r_wr2.append(Res())

        def l0_wload(pi):
            g, ep = pairs[pi]
            s = pi % 3
            for part in range(3):
                k.dma("pool", "wr%d" % s, wr2[s][:, :, part * 256:(part + 1) * 256],
                      cwin_d[:, part * 2048 + ep * 256: part * 2048 + (ep + 1) * 256].rearrange("(k p) c -> p k c", p=128),
                      writes=[r_wr2[s]])

        blocks = [(g, e) for g in range(NG0) for e in range(16)]
        pairs = [(g, ep) for g in range(NG0) for ep in range(8)]
        k.barrier()
        l0_wload(0)
        l0_wload(1)

        mean_ps, r_meanps = banks[0]
        ex2_ps, r_ex2ps = banks[1]

        stage(2)
        l0_xload(0)
        l0_front(0)
        stage(3)
        glo_ln = PK["cln_g"][0]
        bbo_ln = PK["cln_b"][0]
        gT = k.sb("gT", [128, 16, 512], BF16)
        r_gT = [Res() for _ in range(16)]
        pend_ln = [None]
        pend_op = [None]

        def ln_item(e2):
            p2 = e2 % 2
            k.op("pool", lambda e_: e_.tensor_tensor(out=t1[p2][:], in0=cv[:, e2, :], in1=mean_t[:], op=ALU.subtract),
                 reads=[r_cv[e2], r_mean], writes=[r_t1[p2]])
            k.op("dve", lambda e_: e_.tensor_tensor(out=t1[p2][:], in0=t1[p2][:], in1=rstd_t[:], op=ALU.mult),
                 reads=[r_t1[p2], r_rstd], writes=[r_t1[p2]])
            k.op("act", lambda e_: e_.activation(out=ub[p2][:], in_=t1[p2][:], func=AF.Silu,
                                                 scale=pk[:, glo_ln + e2:glo_ln + e2 + 1],
                                                 bias=pk[:, bbo_ln + e2:bbo_ln + e2 + 1]),
                 reads=[r_t1[p2], r_pk], writes=[r_ub[p2]])
            k.op("dve", lambda e_: e_.tensor_tensor(out=gT[:, e2, :], in0=sz[:, e2, :], in1=ub[p2][:], op=ALU.mult),
                 reads=[r_sz[e2], r_ub[p2]], writes=[r_gT[e2]])

        def outproj(g, cond):
            for tt in range(4):
                yi = ytst["i"] % 2
                ytst["i"] += 1
                row0 = g * 512 + tt * 128
                k.dma("sp", "xr%d" % yi, xr[yi][:], x_d[row0:row0 + 128, :], writes=[r_xr[yi]])
                for half in range(2):
                    po, r_po = bank()
                    for e2 in range(16):
                        k.op("pe", lambda e_, e2=e2, po=po: e_.matmul(
                            po[:], lhsT=gT[:, e2, tt * 128:(tt + 1) * 128],
                            rhs=wout[:, e2, half * 512:(half + 1) * 512], start=(e2 == 0), stop=(e2 == 15)),
                            reads=[r_gT[e2], r_wout], writes=[r_po])
                    k.op("dve", lambda e_, po=po: e_.tensor_tensor(
                        out=yt[yi][:, half * 512:(half + 1) * 512], in0=po[:],
                        in1=GT[:, cond, half * 512:(half + 1) * 512], op=ALU.mult),
                        reads=[r_po, r_GT], writes=[r_yt[yi]])
                k.op("pool", lambda e_: e_.tensor_tensor(out=yt[yi][:], in0=yt[yi][:], in1=xr[yi][:], op=ALU.add),
                     reads=[r_yt[yi], r_xr[yi]], writes=[r_yt[yi]])
                dst = y_d if stop_after_l0 else y1_d
                k.dma("sp", "yt%d" % yi, dst[row0:row0 + 128, :], yt[yi][:], reads=[r_yt[yi]])

        vp_layout = [None, None]
        pend0 = [None]
        r_dgs = [Res() for _ in range(16)]
        for nm in ("dgst", "dgld0", "dgld1"):
            k.dsem(nm)
        for bi, (g, e) in enumerate(blocks):
            b = g % 2
            par = bi % 2
            R, W, cond = l0_geom(g)
            Wp = W + 30
            s = (bi // 2) % 3
            if bi % 2 == 0 and bi // 2 + 2 < len(pairs):
                l0_wload(bi // 2 + 2)
            if g == 0:
                k.op("pool", lambda e_, par=par, e=e: e_.tensor_tensor(
                    out=dg[par][:], in0=identb[:].unsqueeze(1).to_broadcast([128, 31, 128]),
                    in1=w05[:, e, :].unsqueeze(2).to_broadcast([128, 31, 128]), op=ALU.mult),
                    reads=[r_idb, r_w05], writes=[r_dg[par]])
                k.dma("sp", "dgst", dgs_d[e], dg[par][:].rearrange("p k j -> p (k j)"), reads=[r_dg[par]],
                      writes=[r_dgs[e]])
            else:
                k.dma("sp", "dgld%d" % par, dg[par][:].rearrange("p k j -> p (k j)"), dgs_d[e], reads=[r_dgs[e]],
                      writes=[r_dg[par]])
            if vp_layout[par] != (R, W):
                k.op("pool", lambda e_, par=par: e_.memset(vp[par][:], 0.0), writes=[r_vp[par]])
                vp_layout[par] = (R, W)
            pa, r_pa = bank()
            pg, r_pg = bank()
            pz, r_pz = bank()
            for (pp, r_pp, part) in ((pa, r_pa, 0), (pg, r_pg, 1), (pz, r_pz, 2)):
                for kk in range(8):
                    c0 = part * 256 + (e % 2) * 128
                    k.op("pe", lambda e_, pp=pp, c0=c0, kk=kk, s=s, b=b: e_.matmul(
                        pp[:], lhsT=wr2[s][:, kk, c0:c0 + 128], rhs=hT[b][:, kk, :],
                        start=(kk == 0), stop=(kk == 7)),
                        reads=[r_wr2[s], r_hT[b]], writes=[r_pp])
            blo = PK["cb_in"][0]
            k.op("act", lambda e_, par=par, pa=pa, e=e: e_.activation(
                out=a_t[par][:], in_=pa[:], func=AF.Identity, bias=pk[:, blo + e:blo + e + 1]),
                reads=[r_pa, r_pk], writes=[r_a[par]])
            k.op("act", lambda e_, par=par, pg=pg, e=e: e_.activation(
                out=th_t[par][:], in_=pg[:], func=AF.Tanh, scale=0.5, bias=bg05[:, e:e + 1]),
                reads=[r_pg, r_w05], writes=[r_th[par]])
            vpv = vp[par][:, 0:R * Wp].rearrange("p (r w) -> p r w", w=Wp)
            k.op("dve", lambda e_, par=par, vpv=vpv, W=W: e_.scalar_tensor_tensor(
                out=vpv[:, :, 15:15 + W], in0=th_t[par][:].rearrange("p (r w) -> p r w", w=W), scalar=1.0,
                in1=a_t[par][:].rearrange("p (r w) -> p r w", w=W), op0=ALU.add, op1=ALU.mult),
                reads=[r_a[par], r_th[par]], writes=[r_vp[par]])
            if pend_ln[0] is not None:
                ln_item(e)
            k.op("act", lambda e_, pz=pz, e=e: e_.activation(
                out=sz[:, e, :], in_=pz[:], func=AF.Silu, bias=pk[:, blo + 32 + e:blo + 33 + e]),
                reads=[r_pz, r_pk], writes=[r_sz[e]])
            def conv_block(e, par, vpv, W):
                pc, r_pc = bank()
                pcv = pc[:].rearrange("p (r w) -> p r w", w=W)
                for t in range(31):
                    k.op("pe", lambda e_, t=t: e_.matmul(
                        pcv, lhsT=dg[par][:, t, :], rhs=vpv[:, :, t:t + W], start=(t == 0), stop=(t == 30)),
                        reads=[r_dg[par], r_vp[par]], writes=[r_pc])
                dlo = PK["cb_dw"][0]
                k.op("act", lambda e_: e_.activation(
                    out=cv[:, e, :], in_=pc[:], func=AF.Identity, bias=pk[:, dlo + e:dlo + e + 1]),
                    reads=[r_pc, r_pk], writes=[r_cv[e]])
            if pend0[0] is not None:
                conv_block(*pend0[0])
            pend0[0] = (e, par, vpv, W)
            if e == 15:
                conv_block(*pend0[0])
                pend0[0] = None
            if e == 8 and g + 1 < NG0:
                l0_xload(g + 1)
            if bi == 0:
                stage(4)
            if bi == 4:
                stage(5)
            if STAGE >= 100 and bi == STAGE - 100:
                stage(STAGE)
            if e == 15:
                for ep in range(16):
                    k.op("pe", lambda e_, ep=ep: e_.matmul(mean_ps[:], lhsT=onesE[:], rhs=cv[:, ep, :],
                                                           start=(ep == 0), stop=(ep == 15)),
                         reads=[r_onesE, r_cv[ep]], writes=[r_meanps])
                for ep in range(16):
                    pq = ep % 2
                    k.op("act", lambda e_, ep=ep, pq=pq: e_.activation(out=sqb[pq][:], in_=cv[:, ep, :], func=AF.Square),
                         reads=[r_cv[ep]], writes=[r_sqb[pq]])
                    k.op("pe", lambda e_, ep=ep, pq=pq: e_.matmul(ex2_ps[:], lhsT=onesE[:], rhs=sqb[pq][:],
                                                                  start=(ep == 0), stop=(ep == 15)),
                         reads=[r_onesE, r_sqb[pq]], writes=[r_ex2ps])
                if pend_op[0] is not None:
                    outproj(*pend_op[0])
                    pend_op[0] = None
                if g + 1 < NG0:
                    l0_front(g + 1)
                k.op("act", lambda e_: e_.activation(out=mean_t[:], in_=mean_ps[:], func=AF.Copy),
                     reads=[r_meanps], writes=[r_mean])
                k.op("pool", lambda e_: e_.tensor_tensor(out=rstd_t[:], in0=mean_t[:], in1=mean_t[:], op=ALU.mult),
                     reads=[r_mean], writes=[r_rstd])
                k.op("dve", lambda e_: e_.tensor_tensor(out=rstd_t[:], in0=ex2_ps[:], in1=rstd_t[:], op=ALU.subtract),
                     reads=[r_ex2ps, r_rstd], writes=[r_rstd])
                k.op("act", lambda e_: e_.activation(out=rstd_t[:], in_=rstd_t[:], func=AF.Ln, bias=epsc[:, :]),
                     reads=[r_rstd, r_eps], writes=[r_rstd])
                k.op("act", lambda e_: e_.activation(out=rstd_t[:], in_=rstd_t[:], func=AF.Exp, scale=-0.5),
                     reads=[r_rstd], writes=[r_rstd])
                if pend_op[0] is not None:
                    outproj(*pend_op[0])
                    pend_op[0] = None
                pend_ln[0] = g
                pend_op[0] = (g, cond)
                if g == NG0 - 1:
                    for e2 in range(16):
                        ln_item(e2)
                    outproj(g, cond)
                    pend_op[0] = None
                    pend_ln[0] = None

        if stop_after_l0:
            k.finish(store_sems)
            k.pop()
            print("instructions", k.nins, "waits", k.nwait)
            return nc
        k.barrier()
        k.pop()

        k.push()
        V = k.op
        adaln(1)
        rotcfg["base"] = 0
        rotcfg["n"] = 8
        r_c = Res()

        def cmat(name, pattern, cm, base):
            t = k.sb(name, [128, 128])
            V("pool", lambda e: e.memset(t[:], 1.0), writes=[r_c])
            if pattern is not None:
                V("pool", lambda e: e.affine_select(out=t[:], in_=t[:], pattern=pattern, compare_op=ALU.is_ge,
                                                    fill=0.0, base=base, channel_multiplier=cm), reads=[r_c], writes=[r_c])
            return t

        TRIle = cmat("TRIle", [[1, 128]], -1, 0)
        TRIge = cmat("TRIge", [[-1, 128]], 1, 0)
        UFf = cmat("UFf", [[-1, 128]], 1, -1)
        UBf = cmat("UBf", [[1, 128]], -1, -1)
        ONESf = cmat("ONESf", None, 0, 0)

        def tobf(name, src):
            t = k.sb(name, [128, 128], BF16)
            V("dve", lambda e: e.tensor_copy(out=t[:], in_=src[:]), reads=[r_c], writes=[r_c])
            return t

        TRIle_b = tobf("TRIle_b", TRIle)
        TRIge_b = tobf("TRIge_b", TRIge)
        UF_b = tobf("UF_b", UFf)
        UB_b = tobf("UB_b", UBf)
        one_c = k.sb("one_c", [128, 1])
        V("pool", lambda e: e.memset(one_c[:], 1.0), writes=[r_c])
        aneg = k.sb("aneg", [128, 64])
        V("act", lambda e: e.activation(out=aneg[:], in_=pkc("alog"), func=AF.Exp), reads=[r_pk], writes=[r_c])
        V("dve", lambda e: e.tensor_scalar(out=aneg[:], in0=aneg[:], scalar1=-1.0, scalar2=0.0, op0=ALU.mult, op1=ALU.add),
          reads=[r_c], writes=[r_c])

        sng = k.sb("sng", [128, 2048], BF16)
        fng = k.sb("fng", [128, 1024])
        wdt = k.sb("wdt", [128, 8, 64], BF16)
        for nm in ("sng", "fng", "wdt", "xw", "xh", "prevb", "spill", "y2", "stout", "xres"):
            k.dsem(nm)
        store_sems.extend(["y2", "stout", "spill"])
        k.dma("pool", "sng", sng[:], sng_d[:, :], writes=[r_c])
        k.dma("sp", "fng", fng[:], fng_d[:, :], writes=[r_c])
        k.dma("pool", "wdt", wdt[:], swin_d[:, 5120:5184].rearrange("(k p) c -> p k c", p=128), writes=[r_c])

        xw = k.sb("xw", [128, 3, 1024]); r_xw = Res()
        xh = xw[:, 2, :]; r_xh = r_xw
        y2 = k.sb("y2", [128, 1024]); r_y2 = Res()
        xres = k.sb("xres", [128, 1024]); r_xres = Res()
        ss1 = k.sb("ss1", [128, 4]); rs1 = k.sb("rs1", [128, 4]); r_ss1 = Res()
        V("pool", lambda e: e.memset(ss1[:], 1.0), writes=[r_ss1])
        hTw2 = [k.sb("hTw%d" % i, [128, 8, 260], BF16) for i in range(2)]; r_hTw2 = [Res(), Res()]
        ub = [k.sb("ub1_%d" % i, [128, 516], BF16) for i in range(2)]; r_ub = [Res(), Res()]
        dg1 = [k.sb("dg1_%d" % i, [128, 5, 128], BF16) for i in range(2)]; r_dg1 = [Res(), Res()]
        xbcT2 = [k.sb("xbcT%d" % i, [128, 24, 256], BF16) for i in range(2)]
        r_xbc2 = [[Res() for _ in range(24)] for _ in range(2)]
        szTc = [k.sb("szT%d" % i, [128, 2048], BF16) for i in range(2)]; r_szTc = [Res(), Res()]
        for nm in ("spx", "spz0", "spz1", "spd", "ldx0", "ldx1", "ldz0", "ldz1", "ldd0", "ldd1"):
            k.dsem(nm)
        store_sems.extend(["spx", "spz0", "spz1", "spd"])
        r_xsp = [Res() for _ in range(10)]; r_zsp = [[Res(), Res()] for _ in range(10)]; r_dsp = [Res() for _ in range(10)]
        dtt2 = [k.sb("dtt%d" % i, [128, 2, 64]) for i in range(2)]; r_dtt2 = [Res(), Res()]
        S_one = k.sb("S_one", [128, 2048]); r_S_one = Res()
        S = [S_one, S_one]; r_S = [r_S_one, r_S_one]
        Sbf = k.sb("Sbf", [128, 2048], BF16); r_Sbf = Res()
        prevb = k.sb("prevb", [128, 2048], BF16); r_prevb = Res()
        r_sbs = [Res() for _ in range(20)]
        xtok = k.sb("xtok", [128, 2048], BF16); r_xtok = Res()
        btok = k.sb("btok", [128, 512], BF16); r_btok = Res()
        la = [k.sb("la%d" % d, [128, 32]) for d in range(2)]
        acs = [k.sb("acs%d" % d, [128, 32]) for d in range(2)]
        ea = [k.sb("ea%d" % d, [128, 32]) for d in range(2)]
        cd = [k.sb("cd%d" % d, [128, 32]) for d in range(2)]
        dte = [k.sb("dte%d" % d, [128, 32]) for d in range(2)]
        wd = [k.sb("wd%d" % d, [128, 32]) for d in range(2)]
        r_dq = [Res(), Res()]
        cbm = [k.sb("cbm%d" % d, [128, 4, 128], BF16) for d in range(2)]; r_cbm = Res()
        Rt = [k.sb("Rt%d" % d, [128, 8, 128], BF16) for d in range(2)]; r_Rt = [Res(), Res()]
        Eg = [[k.sb("Eg%d_%d" % (q, d), [128, 8, 128], BF16) for d in range(2)] for q in range(2)]
        r_Eg = [[Res(), Res()], [Res(), Res()]]
        xdt = [[k.sb("xdt%d_%d" % (q, d), [128, 512], BF16) for d in range(2)] for q in range(2)]
        r_xdt = [[Res(), Res()], [Res(), Res()]]
        xdteB = [k.sb("xdteB%d" % q, [128, 512], BF16) for q in range(2)]; r_xdteB = [Res(), Res()]
        xdte = k.sb("xdte", [128, 512], BF16); r_xdte = Res()
        u1 = k.sb("u1", [128, 512]); u2 = k.sb("u2", [128, 512]); yg = k.sb("yg", [128, 512])
        t3 = [k.sb("t3_%d" % q, [128, 512]) for q in range(2)]
        r_u1 = Res(); r_u2 = Res(); r_t3 = [Res(), Res()]; r_yg = Res()
        ssg = k.sb("ssg", [128, 2]); r_ssg = Res()
        yn = [k.sb("yn%d" % q, [128, 512], BF16) for q in range(2)]; r_yn = [Res(), Res()]
        ynT = k.sb("ynT", [128, 16, 128], BF16); r_ynT = Res()
        tS = k.sb("tS", [128, 512]); r_tS = Res()
        print("L1 sbuf bytes remaining", nc.sbuf_bytes_remaining)

        def bc8(ap32, g):
            return ap32[:, g * 8:(g + 1) * 8].unsqueeze(2).to_broadcast([128, 8, 64])

        def v864(ap512):
            return ap512.rearrange("p (h q) -> p h q", q=64)

        def wstream(srcs):
            st = {"n": 0, "slot": {}}

            def get(i):
                while st["n"] < len(srcs) and st["n"] <= i + 2:
                    sl = wslot()
                    k.dma("pool", "wr%d" % sl, wr[sl], srcs[st["n"]], writes=[r_wr[sl]])
                    st["slot"][st["n"]] = sl
                    st["n"] += 1
                return st["slot"][i]
            return get

        def wsrc(c0):
            return swin_d[:, c0:c0 + 512].rearrange("(k p) c -> p k c", p=128)

        def l1_front(base, nw, w, cond, wb, phase=None):
            hTw = hTw2[wb]; r_hTw = r_hTw2[wb]
            t0 = base + w * 256
            if phase in (None, "a", "a0"):
                k.dma("sp", "xw", xw[:, 0:2, :], y1_d[t0:t0 + 256, :].rearrange("(t p) d -> p t d", p=128), writes=[r_xw])
                V("pool", lambda e: e.memset(xh[0:4, :], 0.0), writes=[r_xh])
                if w > 0:
                    k.dma("sp", "xh", xh[0:2, :], y1_d[t0 - 2:t0, :], writes=[r_xh])
                if w < nw - 1:
                    k.dma("sp", "xh", xh[2:4, :], y1_d[t0 + 256:t0 + 258, :], writes=[r_xh])
            if phase == "a0":
                return
            if phase in (None, "a", "a1"):
                sumsq(xw[:, 0, :], r_xw, 128, ss1[:, 0:1], r_ss1)
                sumsq(xw[:, 1, :], r_xw, 128, ss1[:, 1:2], r_ss1)
                sumsq(xh[0:4, :], r_xh, 4, ss1[0:4, 2:3], r_ss1)
                rstd_of(ss1[:, 0:3], rs1[:, 0:3], 128, r_ss1, 1.0 / D)
            ph2 = "a" if phase == "a1" else phase
            xn_transpose(xw[:, 0, :], r_xw, 128, cond, lambda kq: hTw[:, kq, 0:128], r_hTw, rs1[:, 0:1], r_ss1, ph2)
            xn_transpose(xw[:, 1, :], r_xw, 128, cond, lambda kq: hTw[:, kq, 128:256], r_hTw, rs1[:, 1:2], r_ss1, ph2)
            xn_transpose(xh[0:4, :], r_xh, 4, cond, lambda kq: hTw[:, kq, 256:260], r_hTw, rs1[0:4, 2:3], r_ss1, ph2)
            if ph2 == "a":
                return
            if w == 0:
                V("pool", lambda e: e.memset(hTw[:, :, 256:258], 0.0), writes=[r_hTw])
            if w == nw - 1:
                V("pool", lambda e: e.memset(hTw[:, :, 258:260], 0.0), writes=[r_hTw])

        wclo = PK["sw_conv"][0]
        bclo = PK["sb_conv"][0]

        def l1_xbc_pieces(get, idxs, npieces, wb, only=None):
            hTw = hTw2[wb]; r_hTw = r_hTw2[wb]; xbcT = xbcT2[wb]; r_xbc = r_xbc2[wb]
            def conv_part(j, pj):
                pc, r_pc = bank()
                for t in range(5):
                    V("pe", lambda e, t=t, pj=pj, pc=pc: e.matmul(
                        pc[:, 0:256], lhsT=dg1[pj][:, t, :], rhs=ub[pj][:, t:t + 256], start=(t == 0), stop=(t == 4)),
                      reads=[r_dg1[pj], r_ub[pj]], writes=[r_pc])
                V("act", lambda e, j=j, pc=pc: e.activation(out=xbcT[:, j, :], in_=pc[:, 0:256], func=AF.Silu,
                                                            bias=pk[:, bclo + j:bclo + j + 1]),
                  reads=[r_pc, r_pk], writes=[r_xbc[j]])

            st = {"pend": None}

            def piece(pc_i):
                sl = get(idxs[pc_i])
                for jj in range(4):
                    j = pc_i * 4 + jj
                    pj = j % 2
                    V("dve", lambda e, pj=pj, j=j: e.tensor_tensor(
                        out=dg1[pj][:], in0=identb[:].unsqueeze(1).to_broadcast([128, 5, 128]),
                        in1=pk[:, wclo + j * 5:wclo + j * 5 + 5].unsqueeze(2).to_broadcast([128, 5, 128]), op=ALU.mult),
                      reads=[r_idb, r_pk], writes=[r_dg1[pj]])
                    pb, r_pb = bank()
                    for kk in range(8):
                        V("pe", lambda e, kk=kk, sl=sl, jj=jj, pb=pb: e.matmul(
                            pb[:, 0:260], lhsT=wr[sl][:, kk, jj * 128:(jj + 1) * 128], rhs=hTw[:, kk, :],
                            start=(kk == 0), stop=(kk == 7)), reads=[r_wr[sl], r_hTw], writes=[r_pb])
                    V("act", lambda e, pj=pj, pb=pb: e.activation(out=ub[pj][:, 2:258], in_=pb[:, 0:256], func=AF.Copy),
                      reads=[r_pb], writes=[r_ub[pj]])
                    V("act", lambda e, pj=pj, pb=pb: e.activation(
                        out=ub[pj][:, :].rearrange("p (a b) -> p a b", b=258)[:, :, 0:2],
                        in_=pb[:, 256:260].rearrange("p (a b) -> p a b", b=2), func=AF.Copy),
                      reads=[r_pb], writes=[r_ub[pj]])
                    if st["pend"] is not None:
                        conv_part(*st["pend"])
                    st["pend"] = (j, pj)
                if pc_i == npieces - 1 or only is not None:
                    conv_part(*st["pend"])
                    st["pend"] = None

            return [(lambda pc_i=pc_i: piece(pc_i)) for pc_i in range(npieces)]

        def l1_z_part(get, idx, wb, q):
            hTw = hTw2[wb]; r_hTw = r_hTw2[wb]
            sl = get(idx)
            for cc in range(2):
                pb, r_pb = bank()
                for kk in range(8):
                    V("pe", lambda e, kk=kk, cc=cc, pb=pb: e.matmul(
                        pb[:], lhsT=hTw[:, kk, cc * 128:(cc + 1) * 128], rhs=wr[sl][:, kk, :],
                        start=(kk == 0), stop=(kk == 7)), reads=[r_wr[sl], r_hTw], writes=[r_pb])
                V("act", lambda e, cc=cc, pb=pb: e.activation(
                    out=szTc[cc][:, q * 512:(q + 1) * 512], in_=pb[:], func=AF.Silu), reads=[r_pb], writes=[r_szTc[cc]])

        def l1_dt(wb):
            hTw = hTw2[wb]; r_hTw = r_hTw2[wb]; dtt = dtt2[wb]; r_dtt = r_dtt2[wb]
            for cc in range(2):
                pb, r_pb = bank()
                for kk in range(8):
                    V("pe", lambda e, kk=kk, cc=cc, pb=pb: e.matmul(
                        pb[:, 0:64], lhsT=hTw[:, kk, cc * 128:(cc + 1) * 128], rhs=wdt[:, kk, :],
                        start=(kk == 0), stop=(kk == 7)), reads=[r_c, r_hTw], writes=[r_pb])
                V("dve", lambda e, cc=cc, pb=pb: e.tensor_tensor(out=dtt[:, cc, :], in0=pb[:, 0:64], in1=pkc("dtb"), op=ALU.add),
                  reads=[r_pb, r_pk], writes=[r_dtt])
            dv = dtt[:].rearrange("p c h -> p (c h)")
            V("act", lambda e: e.activation(out=dv, in_=dv, func=AF.Exp), reads=[r_dtt], writes=[r_dtt])
            V("act", lambda e: e.activation(out=dv, in_=dv, func=AF.Ln, bias=one_c[:, :]), reads=[r_dtt, r_c], writes=[r_dtt])

        def tok_major(cc, wb):
            xbcT = xbcT2[wb]; r_xbc = r_xbc2[wb]
            c0 = cc * 128
            for q4 in range(4):
                pb, r_pb = bank()
                for i in range(4):
                    j = q4 * 4 + i
                    V("pe", lambda e, i=i, j=j, pb=pb: e.matmul(
                        pb[:, i * 128:(i + 1) * 128], lhsT=xbcT[:, j, c0:c0 + 128], rhs=identb[:], start=True, stop=True),
                      reads=[r_xbc[j], r_idb], writes=[r_pb])
                V("act", lambda e, q4=q4, pb=pb: e.activation(out=xtok[:, q4 * 512:(q4 + 1) * 512], in_=pb[:], func=AF.Copy),
                  reads=[r_pb], writes=[r_xtok])
            pb, r_pb = bank()
            for g in range(4):
                V("pe", lambda e, g=g, pb=pb: e.matmul(
                    pb[:, g * 128:(g + 1) * 128], lhsT=xbcT[:, 16 + g, c0:c0 + 128], rhs=identb[:], start=True, stop=True),
                  reads=[r_xbc[16 + g], r_idb], writes=[r_pb])
            V("dve", lambda e, pb=pb: e.tensor_copy(out=btok[:], in_=pb[:]), reads=[r_pb], writes=[r_btok])

        def dtq(cc, d, wb):
            dtt = dtt2[wb]; r_dtt = r_dtt2[wb]
            dtd = dtt[:, cc, d * 32:(d + 1) * 32]
            V("dve", lambda e: e.tensor_tensor(out=la[d][:], in0=dtd, in1=aneg[:, d * 32:(d + 1) * 32], op=ALU.mult),
              reads=[r_dtt, r_c], writes=[r_dq[d]])
            pb, r_pb = bank()
            V("pe", lambda e: e.matmul(pb[:, 0:32], lhsT=(TRIle if d == 0 else TRIge)[:], rhs=la[d][:], start=True, stop=True),
              reads=[r_c, r_dq[d]], writes=[r_pb])
            V("pe", lambda e: e.matmul(pb[:, 32:64], lhsT=ONESf[:], rhs=la[d][:], start=True, stop=True),
              reads=[r_c, r_dq[d]], writes=[r_pb])
            V("act", lambda e: e.activation(out=acs[d][:], in_=pb[:, 0:32], func=AF.Identity), reads=[r_pb], writes=[r_dq[d]])
            V("act", lambda e: e.activation(out=ea[d][:], in_=pb[:, 0:32], func=AF.Exp), reads=[r_pb], writes=[r_dq[d]])
            V("act", lambda e: e.activation(out=cd[d][:], in_=pb[:, 32:64], func=AF.Exp), reads=[r_pb], writes=[r_dq[d]])
            V("dve", lambda e: e.tensor_tensor(out=dte[d][:], in0=pb[:, 32:64], in1=acs[d][:], op=ALU.subtract),
              reads=[r_pb, r_dq[d]], writes=[r_dq[d]])
            V("act", lambda e: e.activation(out=dte[d][:], in_=dte[d][:], func=AF.Exp), reads=[r_dq[d]], writes=[r_dq[d]])
            V("dve", lambda e: e.tensor_tensor(out=wd[d][:], in0=dtd, in1=dte[d][:], op=ALU.mult),
              reads=[r_dtt, r_dq[d]], writes=[r_dq[d]])

        def state_update(d, g):
            V("dve", lambda e: e.tensor_tensor(out=v864(xdte[:]), in0=v864(xtok[:, g * 512:(g + 1) * 512]),
                                                in1=bc8(wd[d], g), op=ALU.mult),
              reads=[r_xtok, r_dq[d]], writes=[r_xdte])
            pb, r_pb = bank()
            V("pe", lambda e: e.matmul(pb[:], lhsT=btok[:, g * 128:(g + 1) * 128], rhs=xdte[:], start=True, stop=True),
              reads=[r_btok, r_xdte], writes=[r_pb])
            V("dve", lambda e: e.tensor_tensor(out=v864(tS[:]), in0=v864(S[d][:, g * 512:(g + 1) * 512]),
                                               in1=bc8(cd[d], g), op=ALU.mult),
              reads=[r_S[d], r_dq[d]], writes=[r_tS])
            V("dve", lambda e: e.tensor_tensor(out=S[d][:, g * 512:(g + 1) * 512], in0=pb[:], in1=tS[:], op=ALU.add),
              reads=[r_pb, r_tS], writes=[r_S[d]])

        stv = xw[:, 0:2, :].rearrange("p t (j n) -> p (t j) n", n=128)

        def state_load(d, src):
            k.dma("sp", "xw", stv, src.rearrange("(j p) n -> p j n", p=128), writes=[r_xw])
            for q4 in range(4):
                pb, r_pb = bank()
                for i in range(4):
                    j = q4 * 4 + i
                    V("pe", lambda e, i=i, j=j, pb=pb: e.transpose(out=pb[:, i * 128:(i + 1) * 128], in_=stv[:, j, :],
                                                                   identity=identf),
                      reads=[r_xw, r_pk], writes=[r_pb])
                V("act", lambda e, q4=q4, pb=pb: e.activation(out=S[d][:, q4 * 512:(q4 + 1) * 512], in_=pb[:], func=AF.Copy),
                  reads=[r_pb], writes=[r_S[d]])

        def state_store(d, dst):
            for q4 in range(4):
                pb, r_pb = bank()
                for i in range(4):
                    j = q4 * 4 + i
                    V("pe", lambda e, i=i, j=j, pb=pb: e.transpose(out=pb[:, i * 128:(i + 1) * 128],
                                                                   in_=S[d][:, j * 128:(j + 1) * 128], identity=identf),
                      reads=[r_S[d], r_pk], writes=[r_pb])
                V("act", lambda e, q4=q4, pb=pb: e.activation(
                    out=stv[:, q4 * 4:(q4 + 1) * 4, :], in_=pb[:].rearrange("p (j n) -> p j n", n=128), func=AF.Copy),
                  reads=[r_pb], writes=[r_xw])
            k.dma("sp", "stout", dst.rearrange("(j p) n -> p j n", p=128), stv, reads=[r_xw])

        def chunkA(cc, cid, wb, slots=()):
            slots = list(slots)
            V("act", lambda e: e.activation(out=Sbf[:], in_=S[1][:], func=AF.Copy), reads=[r_S[1]], writes=[r_Sbf])
            k.dma("sp", "spill", sbs_d[cid], Sbf[:], reads=[r_Sbf], writes=[r_sbs[cid]])
            dtq(cc, 1, wb)
            tok_major(cc, wb)
            if slots:
                slots.pop(0)()
            for g in range(4):
                state_update(1, g)
                if g in (0, 2) and slots:
                    slots.pop(0)()
            while slots:
                slots.pop(0)()

        nglo = 0

        def chunkB(cc, cid, row0, cond, wb, pre_fns, mid_fns):
            c0 = cc * 128
            xbcT = xbcT2[wb]; r_xbc = r_xbc2[wb]; dtt = dtt2[wb]; r_dtt = r_dtt2[wb]
            k.dma("sp", "xres", xres[:], y1_d[row0:row0 + 128, :], writes=[r_xres])
            k.dma("sp", "prevb", prevb[:], sbs_d[cid], reads=[r_sbs[cid]], writes=[r_prevb])
            V("act", lambda e: e.activation(out=Sbf[:], in_=S[0][:], func=AF.Copy), reads=[r_S[0]], writes=[r_Sbf])
            dtq(cc, 0, wb)
            dtq(cc, 1, wb)
            tok_major(cc, wb)
            pcb, r_pcb = bank()
            for g in range(4):
                V("pe", lambda e, g=g: e.matmul(pcb[:, g * 128:(g + 1) * 128], lhsT=xbcT[:, 16 + g, c0:c0 + 128],
                                                rhs=xbcT[:, 20 + g, c0:c0 + 128], start=True, stop=True),
                  reads=[r_xbc[16 + g], r_xbc[20 + g]], writes=[r_pcb])
            for d, M in ((0, TRIle_b), (1, TRIge_b)):
                V("dve", lambda e, d=d, M=M: e.tensor_tensor(
                    out=cbm[d][:], in0=pcb[:].rearrange("p (g l) -> p g l", l=128),
                    in1=M[:].unsqueeze(1).to_broadcast([128, 4, 128]), op=ALU.mult),
                  reads=[r_pcb, r_c], writes=[r_cbm])
            def S1(g):
                pq = g % 2
                for d in range(2):
                    Mb = TRIle_b if d == 0 else TRIge_b
                    Ub = UF_b if d == 0 else UB_b
                    V("pool", lambda e, d=d, Mb=Mb: e.tensor_tensor(
                        out=Rt[d][:], in0=Mb[:].unsqueeze(1).to_broadcast([128, 8, 128]),
                        in1=la[d][:, g * 8:(g + 1) * 8].unsqueeze(2).to_broadcast([128, 8, 128]), op=ALU.mult),
                      reads=[r_c, r_dq[d]], writes=[r_Rt[d]])
                    for hh in range(2):
                        pb, r_pb = bank()
                        V("pe", lambda e, d=d, hh=hh, pb=pb, Ub=Ub: e.matmul(
                            pb[:], lhsT=Ub[:], rhs=Rt[d][:, hh * 4:(hh + 1) * 4, :].rearrange("p h l -> p (h l)"),
                            start=True, stop=True), reads=[r_c, r_Rt[d]], writes=[r_pb])
                        V("act", lambda e, d=d, hh=hh, pb=pb: e.activation(
                            out=Eg[pq][d][:, hh * 4:(hh + 1) * 4, :].rearrange("p h l -> p (h l)"), in_=pb[:], func=AF.Exp),
                          reads=[r_pb], writes=[r_Eg[pq][d]])
                    V("dve", lambda e, d=d: e.tensor_tensor(
                        out=Eg[pq][d][:], in0=Eg[pq][d][:], in1=cbm[d][:, g, :].unsqueeze(1).to_broadcast([128, 8, 128]),
                        op=ALU.mult), reads=[r_Eg[pq][d], r_cbm], writes=[r_Eg[pq][d]])
                    V("dve" if d == 0 else "pool", lambda e, d=d: e.tensor_tensor(
                        out=v864(xdt[pq][d][:]), in0=v864(xtok[:, g * 512:(g + 1) * 512]),
                        in1=bc8(dtt[:, cc, d * 32:(d + 1) * 32], g), op=ALU.mult),
                      reads=[r_xtok, r_dtt], writes=[r_xdt[pq][d]])
                V("pool", lambda e: e.tensor_tensor(out=v864(t3[pq][:]), in0=v864(xtok[:, g * 512:(g + 1) * 512]),
                                                    in1=bc8(pkc("sd"), g), op=ALU.mult),
                  reads=[r_xtok, r_pk], writes=[r_t3[pq]])

            def S2a(g):
                pq = g % 2
                if pre_fns.get(g) is not None:
                    pre_fns[g]()
                pyd, r_pyd = bank()
                for h in range(8):
                    for d in range(2):
                        V("pe", lambda e, h=h, d=d: e.matmul(
                            pyd[:, h * 64:(h + 1) * 64], lhsT=Eg[pq][d][:, h, :], rhs=xdt[pq][d][:, h * 64:(h + 1) * 64],
                            start=(d == 0), stop=(d == 1)), reads=[r_Eg[pq][d], r_xdt[pq][d]], writes=[r_pyd])
                pof, r_pof = bank()
                V("pe", lambda e: e.matmul(pof[:], lhsT=xbcT[:, 20 + g, c0:c0 + 128], rhs=Sbf[:, g * 512:(g + 1) * 512],
                                           start=True, stop=True), reads=[r_xbc[20 + g], r_Sbf], writes=[r_pof])
                pob, r_pob = bank()
                V("pe", lambda e: e.matmul(pob[:], lhsT=xbcT[:, 20 + g, c0:c0 + 128], rhs=prevb[:, g * 512:(g + 1) * 512],
                                           start=True, stop=True), reads=[r_xbc[20 + g], r_prevb], writes=[r_pob])
                V("dve", lambda e: e.tensor_tensor(out=v864(u1[:]), in0=v864(pof[:]), in1=bc8(ea[0], g), op=ALU.mult),
                  reads=[r_pof, r_dq[0]], writes=[r_u1])
                V("dve", lambda e: e.tensor_tensor(out=v864(u2[:]), in0=v864(pob[:]), in1=bc8(ea[1], g), op=ALU.mult),
                  reads=[r_pob, r_dq[1]], writes=[r_u2])
                V("dve", lambda e: e.tensor_tensor(out=u1[:], in0=u1[:], in1=u2[:], op=ALU.add),
                  reads=[r_u1, r_u2], writes=[r_u1])
                V("dve", lambda e: e.tensor_tensor(out=u1[:], in0=u1[:], in1=t3[pq][:], op=ALU.add),
                  reads=[r_u1, r_t3[pq]], writes=[r_u1])
                V("pool", lambda e: e.tensor_tensor(out=v864(xdteB[pq][:]), in0=v864(xtok[:, g * 512:(g + 1) * 512]),
                                                    in1=bc8(wd[0], g), op=ALU.mult),
                  reads=[r_xtok, r_dq[0]], writes=[r_xdteB[pq]])
                V("dve", lambda e: e.tensor_tensor(out=yg[:], in0=pyd[:], in1=u1[:], op=ALU.add),
                  reads=[r_pyd, r_u1], writes=[r_yg])
                V("dve", lambda e: e.tensor_tensor(out=yg[:], in0=yg[:], in1=szTc[cc][:, g * 512:(g + 1) * 512], op=ALU.mult),
                  reads=[r_yg, r_szTc[cc]], writes=[r_yg])
                if mid_fns.get(g) is not None:
                    mid_fns[g]()
                V("act", lambda e: e.activation(out=junk[:, 0:512], in_=yg[:], func=AF.Square, accum_out=ssg[:, 0:1]),
                  reads=[r_yg], writes=[r_junk, r_ssg])
                rstd_of(ssg[:, 0:1], ssg[:, 1:2], 128, r_ssg, 1.0 / 512)
                V("dve", lambda e: e.scalar_tensor_tensor(
                    out=yn[pq][:], in0=yg[:], scalar=ssg[:, 1:2], in1=sng[:, g * 512:(g + 1) * 512],
                    op0=ALU.mult, op1=ALU.mult), reads=[r_yg, r_ssg, r_c], writes=[r_yn[pq]])

            def S2b(g):
                pq = g % 2
                pb, r_pb = bank()
                for i in range(4):
                    V("pe", lambda e, i=i, pb=pb: e.matmul(pb[:, i * 128:(i + 1) * 128], lhsT=yn[pq][:, i * 128:(i + 1) * 128],
                                                            rhs=identb[:], start=True, stop=True),
                      reads=[r_yn[pq], r_idb], writes=[r_pb])
                V("act", lambda e, pb=pb: e.activation(
                    out=ynT[:, g * 4:(g + 1) * 4, :], in_=pb[:].rearrange("p (j t) -> p j t", t=128), func=AF.Copy),
                  reads=[r_pb], writes=[r_ynT])
                ps_, r_ps = bank()
                V("pe", lambda e: e.matmul(ps_[:], lhsT=btok[:, g * 128:(g + 1) * 128], rhs=xdteB[pq][:], start=True, stop=True),
                  reads=[r_btok, r_xdteB[pq]], writes=[r_ps])
                V("pool", lambda e: e.tensor_tensor(out=v864(tS[:]), in0=v864(S[0][:, g * 512:(g + 1) * 512]),
                                                    in1=bc8(cd[0], g), op=ALU.mult),
                  reads=[r_S[0], r_dq[0]], writes=[r_tS])
                V("dve", lambda e: e.tensor_tensor(out=S[0][:, g * 512:(g + 1) * 512], in0=ps_[:], in1=tS[:], op=ALU.add),
                  reads=[r_ps, r_tS], writes=[r_S[0]])

            for step in ("1:0", "1:1", "a:0", "1:2", "a:1", "b:0", "1:3", "a:2", "b:1", "a:3", "b:2", "b:3"):
                kind, gi = step.split(":")
                {"1": S1, "a": S2a, "b": S2b}[kind](int(gi))
            for half in range(2):
                po, r_po = bank()
                for e2 in range(16):
                    V("pe", lambda e, e2=e2, half=half, po=po: e.matmul(
                        po[:], lhsT=ynT[:, e2, :], rhs=wout[:, e2, half * 512:(half + 1) * 512],
                        start=(e2 == 0), stop=(e2 == 15)), reads=[r_ynT, r_wout], writes=[r_po])
                V("dve", lambda e, half=half, po=po: e.tensor_tensor(
                    out=y2[:, half * 512:(half + 1) * 512], in0=po[:], in1=GT[:, cond, half * 512:(half + 1) * 512], op=ALU.mult),
                  reads=[r_po, r_GT], writes=[r_y2])
            V("pool", lambda e: e.tensor_tensor(out=y2[:], in0=y2[:], in1=xres[:], op=ALU.add),
              reads=[r_y2, r_xres], writes=[r_y2])
            V("act", lambda e: e.activation(out=junk[:], in_=y2[:], func=AF.Square, accum_out=ssg[:, 0:1]),
              reads=[r_y2], writes=[r_junk, r_ssg])
            rstd_of(ssg[:, 0:1], ssg[:, 1:2], 128, r_ssg, 1.0 / D)
            V("dve", lambda e: e.scalar_tensor_tensor(out=y2[:], in0=y2[:], scalar=ssg[:, 1:2], in1=fng[:],
                                                      op0=ALU.mult, op1=ALU.mult),
              reads=[r_y2, r_ssg, r_c], writes=[r_y2])
            k.dma("sp", "y2", y_d[row0:row0 + 128, :], y2[:], reads=[r_y2])

        seqs = [(0, 8, 0, True, None, 0), (2048, 1, 1, False, 0, 16), (2304, 1, 1, False, 1, 18)]
        if STAGE >= 200:
            seqs = seqs[STAGE - 200:STAGE - 199]
        def spill_window(gw, wb):
            k.dma("sp", "spx", xsp_d[gw], xbcT2[wb][:].rearrange("p j t -> p (j t)"), reads=r_xbc2[wb], writes=[r_xsp[gw]])
            for cc in range(2):
                k.dma("sp", "spz%d" % cc, zsp_d[gw, cc], szTc[cc][:], reads=[r_szTc[cc]], writes=[r_zsp[gw][cc]])
            k.dma("sp", "spd", dsp_d[gw], dtt2[wb][:].rearrange("p c h -> p (c h)"), reads=[r_dtt2[wb]], writes=[r_dsp[gw]])

        def load_xd(gw, wb):
            k.dma("sp", "ldx%d" % wb, xbcT2[wb][:].rearrange("p j t -> p (j t)"), xsp_d[gw], reads=[r_xsp[gw]], writes=r_xbc2[wb])
            k.dma("sp", "ldd%d" % wb, dtt2[wb][:].rearrange("p c h -> p (c h)"), dsp_d[gw], reads=[r_dsp[gw]], writes=[r_dtt2[wb]])

        def load_z(gw, cc):
            k.dma("sp", "ldz%d" % cc, szTc[cc][:], zsp_d[gw, cc], reads=[r_zsp[gw][cc]], writes=[r_szTc[cc]])

        def make_prepA(base, nw, cond, w, wb, getter, i0, gw, skip_front=False, next_front=None):
            pcs = l1_xbc_pieces(getter, [i0 + i for i in range(6)], 6, wb)

            def f0():
                if not skip_front:
                    l1_front(base, nw, w, cond, wb)
                pcs[0]()
                pcs[1]()

            def f1():
                if next_front is not None:
                    next_front("a0")
                pcs[2]()
                pcs[3]()

            def f2():
                pcs[4]()
                pcs[5]()
                l1_dt(wb)

            def f3():
                l1_z_part(getter, i0 + 6, wb, 0)
                l1_z_part(getter, i0 + 7, wb, 1)

            def f4():
                l1_z_part(getter, i0 + 8, wb, 2)
                if next_front is not None:
                    next_front("a1")

            def f5():
                l1_z_part(getter, i0 + 9, wb, 3)
                spill_window(gw, wb)
                if next_front is not None:
                    next_front("b")
            return [f0, f1, f2, f3, f4, f5]

        def win_srcs():
            return [wsrc(2048 + pc * 512) for pc in range(6)] + [wsrc(q * 512) for q in range(4)]

        par0 = 0
        preA_done = False
        getA_next = None
        for si, (base, nw, cond, has_init, oidx, cid0) in enumerate(seqs):
            nxt_seq = seqs[si + 1] if si + 1 < len(seqs) else None
            gw0 = cid0 // 2
            def wbA(w, par0=par0, nw=nw):
                return (par0 + nw - 1 - w) % 2

            first_w = nw - 1
            if preA_done:
                getA = getA_next
                posA = {w: 10 * (first_w - w) for w in range(nw)}
            else:
                getA = wstream([src for _w in range(nw) for src in win_srcs()])
                posA = {w: 10 * (first_w - w) for w in range(nw)}

            def prepA(w, base=base, nw=nw, cond=cond):
                nf = (lambda ph: l1_front(base, nw, w - 1, cond, wbA(w - 1), ph)) if w > 0 else None
                return make_prepA(base, nw, cond, w, wbA(w), getA, posA[w], gw0 + w,
                                  skip_front=(w != first_w), next_front=nf)

            if not preA_done:
                for fn in prepA(first_w):
                    fn()
            if has_init:
                state_load(1, stb_d)
            else:
                V("pool", lambda e: e.memset(S[1][:], 0.0), writes=[r_S[1]])
            if si == 0:
                load_wout(swout_d)
            for w in reversed(range(nw)):
                nxt = prepA(w - 1) if w > 0 else []
                chunkA(1, cid0 + 2 * w + 1, wbA(w), nxt[0:3])
                chunkA(0, cid0 + 2 * w, wbA(w), nxt[3:6])
            if oidx is not None:
                state_store(1, nsb_d[oidx])
            if has_init:
                state_load(0, stf_d)
            else:
                V("pool", lambda e: e.memset(S[0][:], 0.0), writes=[r_S[0]])

            def wbB(w, wbA=wbA):
                return (wbA(0) + w) % 2

            par0_next = 1 - wbB(nw - 1)
            if nxt_seq is not None:
                getA_next = wstream([src for _w in range(nxt_seq[1]) for src in win_srcs()])
            load_xd(gw0, wbB(0))
            load_z(gw0, 0)
            load_z(gw0, 1)
            for w in range(nw):
                wb = wbB(w)
                mid0 = {}
                mid1 = {}
                if w + 1 < nw:
                    mid0 = {1: (lambda w=w: load_xd(gw0 + w + 1, wbB(w + 1)))}
                    mid1 = {0: (lambda w=w: load_z(gw0 + w + 1, 0))}
                elif nxt_seq is not None:
                    nb, nnw, ncond, ncid0 = nxt_seq[0], nxt_seq[1], nxt_seq[2], nxt_seq[5]
                    nnf = None
                    if nnw > 1:
                        nnf = (lambda ph: l1_front(nb, nnw, nnw - 2, ncond, 1 - par0_next, ph))
                    nxa = make_prepA(nb, nnw, ncond, nnw - 1, par0_next, getA_next, 0, ncid0 // 2 + nnw - 1,
                                     next_front=nnf)
                    mid0 = {0: nxa[0], 2: nxa[1]}
                    mid1 = {0: nxa[2], 1: nxa[3], 2: nxa[4], 3: nxa[5]}
                chunkB(0, cid0 + 2 * w, base + w * 256, cond, wb, {}, mid0)
                chunkB(1, cid0 + 2 * w + 1, base + w * 256 + 128, cond, wb, {}, mid1)
                if w + 1 < nw:
                    load_z(gw0 + w + 1, 1)
            if oidx is not None:
                state_store(0, nsf_d[oidx])
            preA_done = nxt_seq is not None
            par0 = par0_next
        k.barrier()
        k.pop()

        k.finish(store_sems)
        print("instructions", k.nins, "waits", k.nwait)
    return nc


def _pack_inputs(inp):
    f = lambda a: np.ascontiguousarray(np.asarray(a, dtype=np.float32))

    def colmajor(v, nblk):
        return f(v).reshape(nblk, 128).T

    common = np.zeros((128, PK_N), np.float32)

    def put(name, arr):
        lo, hi = PK[name]
        common[:, lo:hi] = arr

    put("ada_b", np.concatenate([colmajor(inp["ada_b"][l], 24) for l in range(2)], axis=1))
    put("norm_g", np.concatenate([colmajor(inp["norm_g"][l], 8) for l in range(2)], axis=1))
    put("cb_in", colmajor(inp["conv_b_in"][0], 48))
    wdw = f(inp["conv_w_dw"][0])
    put("cw_dw", wdw.reshape(31, 16, 128).transpose(2, 1, 0).reshape(128, 16 * 31))
    put("cb_dw", colmajor(inp["conv_b_dw"][0], 16))
    put("cln_g", colmajor(inp["conv_ln_g"][0], 16))
    put("cln_b", colmajor(inp["conv_ln_b"][0], 16))
    wc = f(inp["ssd_w_conv"][0])
    put("sw_conv", wc.reshape(5, 24, 128).transpose(2, 1, 0).reshape(128, 24 * 5))
    put("sb_conv", colmajor(inp["ssd_b_conv"][0], 24))
    put("dtb", np.broadcast_to(np.concatenate([f(inp["ssd_dt_bias_f"][0]), f(inp["ssd_dt_bias_b"][0])])[None, :], (128, 64)))
    put("alog", np.broadcast_to(np.concatenate([f(inp["ssd_a_log_f"][0]), f(inp["ssd_a_log_b"][0])])[None, :], (128, 64)))
    put("sd", np.broadcast_to(f(inp["ssd_d"][0])[None, :], (128, 32)))
    put("ident", np.eye(128, dtype=np.float32))
    ada_bg = f(np.broadcast_to(np.concatenate([f(inp["ada_b"][l][2048:3072]) for l in range(2)])[None, :], (128, 2048)))
    sng = f(np.broadcast_to(f(inp["ssd_norm_g"][0])[None, :], (128, 2048)))
    fng = f(np.broadcast_to(f(inp["final_norm_g"])[None, :], (128, 1024)))
    shared = {
        "ada_bg": ada_bg, "sng": sng, "fng": fng,
        "ada_w": f(inp["ada_w"]), "cw_in": f(inp["conv_w_in"][0]), "cw_out": f(inp["conv_w_out"][0]),
        "sw_in": f(inp["ssd_w_in"][0]), "sw_out": f(inp["ssd_w_out"][0]),
    }
    xs = f(inp["x_sample"])
    xp = f(inp["x_prompt"])
    maps = []
    for i in range(NCORES):
        pk = common.copy()
        cond = np.stack([f(inp["c"][i]), f(inp["c_ctx"])], axis=1)
        lo, hi = PK["cond"]
        pk[:, lo:hi] = cond.reshape(8, 128, 2).transpose(1, 0, 2).reshape(128, 16)
        m = dict(shared)
        m["pk"] = pk
        m["x"] = np.concatenate([xs[i], xp[2 * i], xp[2 * i + 1]], axis=0)
        m["stf"] = f(inp["state_fwd"][i, 0]).reshape(2048, 128)
        m["stb"] = f(inp["state_bwd"][i, 0]).reshape(2048, 128)
        maps.append(m)
    return maps


_CACHE = {}


def kernel(**inputs):
    stop = bool(int(os.environ.get("K_STOP_L0", "0")))
    maps = _pack_inputs(inputs)
    nc = build_program(stop_after_l0=stop)
    ndbg = int(os.environ.get("K_NDBG", "0"))
    if ndbg:
        res = run_bass_kernel_spmd(nc, maps[:ndbg], core_ids=list(range(ndbg)))
        r = list(res.results) + [res.results[0]] * (NCORES - ndbg)
    else:
        res = run_bass_kernel_spmd(nc, maps, core_ids=list(range(NCORES)))
        r = res.results
    y_s = np.stack([r[i]["y"][:2048] for i in range(NCORES)], axis=0)
    y_p = np.concatenate([r[i]["y"][2048:].reshape(2, 256, D) for i in range(NCORES)], axis=0)
    nsf = np.concatenate([r[i]["nsf"].reshape(2, 1, 32, 64, 128) for i in range(NCORES)], axis=0)
    nsb = np.concatenate([r[i]["nsb"].reshape(2, 1, 32, 64, 128) for i in range(NCORES)], axis=0)
    return (y_p.astype(np.float32), y_s.astype(np.float32), nsf.astype(np.float32), nsb.astype(np.float32))
```

```python
import contextlib
import os
import numpy as np
import concourse.bass as bass
import concourse.mybir as mybir
from concourse.bass_utils import run_bass_kernel_spmd

F32 = mybir.dt.float32
BF16 = mybir.dt.bfloat16
AF = mybir.ActivationFunctionType
ALU = mybir.AluOpType

NCORES = 8
D = 1024
E = 2048
NTOK = 2560
EPS = 1e-6


class Res:
    __slots__ = ("w", "r")

    def __init__(self):
        self.w = {}
        self.r = {}


class KB:
    def __init__(self, nc, es):
        self.nc = nc
        self.es = es
        self.eng = {"pe": nc.tensor, "act": nc.scalar, "dve": nc.vector,
                    "pool": nc.gpsimd, "sp": nc.sync}
        self.semh = {}
        self.cnt = {}
        for e in ("pe", "act", "dve", "pool"):
            self.semh[e] = es.enter_context(nc.semaphore("s_" + e))
            self.cnt[e] = 0
        self.seen = {e: {} for e in self.eng}
        self.nwait = 0
        self.nins = 0
        self.stacks = [es]

    def push(self):
        st = contextlib.ExitStack()
        self.stacks.append(st)

    def pop(self):
        self.stacks.pop().close()

    def barrier(self):
        for e in self.eng:
            self._wait(e, {s: v for s, v in self.cnt.items() if v > 0 and s != e})

    def sb(self, name, shape, dt=F32):
        return self.stacks[-1].enter_context(self.nc.sbuf_tensor("sb_" + name, list(shape), dt))

    def ps(self, name, shape, dt=F32):
        return self.es.enter_context(self.nc.psum_tensor("ps_" + name, list(shape), dt))

    def dsem(self, name):
        self.semh[name] = self.es.enter_context(self.nc.semaphore("d_" + name))
        self.cnt[name] = 0
        return name

    def _wait(self, e, deps):
        for s, v in deps.items():
            if self.seen[e].get(s, 0) < v:
                self.eng[e].wait_ge(self.semh[s], v)
                self.seen[e][s] = v
                self.nwait += 1

    @staticmethod
    def _deps(e, reads, writes):
        deps = {}
        for r in reads:
            for s, v in r.w.items():
                if e == "pe" and s == "pe":
                    continue
                if deps.get(s, 0) < v:
                    deps[s] = v
        for w in writes:
            for s, v in w.w.items():
                if s == e:
                    continue
                if deps.get(s, 0) < v:
                    deps[s] = v
            for s, v in w.r.items():
                if s == e:
                    continue
                if deps.get(s, 0) < v:
                    deps[s] = v
        return deps

    def op(self, e, fn, reads=(), writes=()):
        self._wait(e, self._deps(e, reads, writes))
        ins = fn(self.eng[e])
        self.cnt[e] += 1
        c = self.cnt[e]
        ins.then_inc(self.semh[e], 1)
        self.nins += 1
        for r in reads:
            r.r[e] = c
        for w in writes:
            w.w[e] = c
            w.r = {}
        return ins

    def dma(self, q, sem, out, in_, reads=(), writes=(), **kw):
        self._wait(q, self._deps(q, reads, writes))
        self.cnt[sem] += 16
        c = self.cnt[sem]
        self.eng[q].dma_start(out=out, in_=in_, **kw).then_inc(self.semh[sem], 16)
        self.nins += 1
        for r in reads:
            r.r[sem] = c
        for w in writes:
            w.w[sem] = c
            w.r = {}

    def finish(self, sems):
        for s in sems:
            if self.cnt[s] > 0:
                self.nc.sync.wait_ge(self.semh[s], self.cnt[s])


PK = {}
_off = 0
for _n, _w in (("cond", 16), ("ada_b", 48), ("norm_g", 16), ("cb_in", 48), ("cw_dw", 496),
               ("cb_dw", 16), ("cln_g", 16), ("cln_b", 16), ("sw_conv", 120), ("sb_conv", 24),
               ("dtb", 64), ("alog", 64), ("sd", 32), ("ident", 128)):
    PK[_n] = (_off, _off + _w)
    _off += _w
PK_N = _off


class _Stop(Exception):
    pass


def build_program(stop_after_l0=False):
    nc = bass.Bass("TRN2", target_bir_lowering=False)
    stage_n = int(os.environ.get("K_STAGE", "0"))
    try:
        _emit(nc, stop_after_l0, stage_n)
    except _Stop:
        pass
    return nc


def _emit(nc, stop_after_l0, STAGE):

    def din(name, shape):
        return nc.dram_tensor(name, list(shape), F32, kind="ExternalInput").ap()

    x_d = din("x", [NTOK, D])
    stf_d = din("stf", [2048, 128])
    stb_d = din("stb", [2048, 128])
    pk_d = din("pk", [128, PK_N])
    adabg_d = din("ada_bg", [128, 2048])
    sng_d = din("sng", [128, 2048])
    fng_d = din("fng", [128, 1024])
    adaw_d = din("ada_w", [2, 1024, 3072])
    cwin_d = din("cw_in", [1024, 6144])
    cwout_d = din("cw_out", [2048, 1024])
    swin_d = din("sw_in", [1024, 5184])
    swout_d = din("sw_out", [2048, 1024])
    y_d = nc.dram_tensor("y", [NTOK, D], F32, kind="ExternalOutput").ap()
    nsf_d = nc.dram_tensor("nsf", [2, 2048, 128], F32, kind="ExternalOutput").ap()
    nsb_d = nc.dram_tensor("nsb", [2, 2048, 128], F32, kind="ExternalOutput").ap()
    y1_d = nc.dram_tensor("y1", [NTOK, D], F32, kind="Internal").ap()
    dbg_d = nc.dram_tensor("dbg", [12, 128, 1024], F32, kind="ExternalOutput").ap() if STAGE else None
    sbs_d = nc.dram_tensor("sbs", [20, 128, 2048], BF16, kind="Internal").ap()
    dgs_d = nc.dram_tensor("dgs", [16, 128, 31 * 128], BF16, kind="Internal").ap()
    xsp_d = nc.dram_tensor("xsp", [10, 128, 24 * 256], BF16, kind="Internal").ap()
    zsp_d = nc.dram_tensor("zsp", [10, 2, 128, 2048], BF16, kind="Internal").ap()
    dsp_d = nc.dram_tensor("dsp", [10, 128, 128], F32, kind="Internal").ap()

    with contextlib.ExitStack() as es:
        k = KB(nc, es)
        store_sems = []

        if STAGE:
            store_sems.append(k.dsem("dbg"))

        def dump(i, ap, w, reads):
            if STAGE:
                k.dma("pool", "dbg", dbg_d[i, :, 0:w], ap, reads=reads)

        def stage(n):
            if STAGE == n:
                k.barrier()
                while len(k.stacks) > 1:
                    k.pop()
                k.finish(store_sems)
                print("STAGE stop", n, "instructions", k.nins, "waits", k.nwait)
                raise _Stop()

        pk = k.sb("pk", [128, PK_N])
        r_pk = Res()
        k.dsem("pk")
        k.dma("sp", "pk", pk[:], pk_d[:, :], writes=[r_pk])

        def pkc(name, a=0, b=None):
            lo, hi = PK[name]
            return pk[:, lo + a: (lo + b) if b is not None else hi]

        identf = pkc("ident")
        identb = k.sb("identb", [128, 128], BF16)
        r_idb = Res()
        k.op("dve", lambda e: e.tensor_copy(out=identb[:], in_=identf), reads=[r_pk], writes=[r_idb])
        onesE = k.sb("onesE", [128, 128], BF16)
        r_onesE = Res()
        k.op("pool", lambda e: e.memset(onesE[:], 1.0 / E), writes=[r_onesE])

        banks = [(k.ps("bank%d" % i, [128, 512]), Res()) for i in range(8)]
        rot = {"i": 0}

        rotcfg = {"base": 2, "n": 6}

        def bank():
            b = banks[rotcfg["base"] + rot["i"] % rotcfg["n"]]
            rot["i"] += 1
            return b

        NS = 3
        wflat = k.sb("wflat", [128, 3, 8, 512], BF16)
        wr = [wflat[:, i] for i in range(NS)]
        wflat2 = wflat[:].rearrange("p s k c -> p (s k c)").rearrange("p (s k c) -> p s k c", s=2, k=8)
        wr2 = [wflat2[:, i] for i in range(2)]
        r_wr = [Res() for _ in range(NS)]
        r_wr2 = [Res() for _ in range(2)]
        for i in range(NS):
            k.dsem("wr%d" % i)
        wstate = {"i": 0}

        def wslot():
            i = wstate["i"] % NS
            wstate["i"] += 1
            return i

        Gt = k.sb("Gt", [128, 8, 2])
        SHt = k.sb("SHt", [128, 8, 2])
        r_G = Res()
        GT = k.sb("GT", [128, 2, 1024], BF16)
        r_GT = Res()
        k.dsem("adabg")

        def adaln(l):
            k.push()
            r_adabg = Res()
            r_msb = Res()
            r_scb = Res()
            r_screp = Res()
            adabg = k.sb("adabg%d" % l, [128, 1024])
            msb = k.sb("msb%d" % l, [128, 16, 2])
            scb = k.sb("scb%d" % l, [128, 8, 2], BF16)
            screp = k.sb("screp%d" % l, [128, 2, 8, 128], BF16)
            k.op("act", lambda e: e.activation(out=scb[:].rearrange("p k c -> p (k c)"), in_=pkc("cond"), func=AF.Silu),
                 reads=[r_pk], writes=[r_scb])
            for c in range(2):
                k.op("dve", lambda e, c=c: e.tensor_copy(out=screp[:, c, :, :],
                                                         in_=scb[:, :, c:c + 1].to_broadcast([128, 8, 128])),
                     reads=[r_scb], writes=[r_screp])
            k.dma("sp", "adabg", adabg[:], adabg_d[:, l * 1024:(l + 1) * 1024], writes=[r_adabg])
            pm, r_pm = banks[0]
            for q in range(6):
                s = wslot()
                k.dma("pool", "wr%d" % s, wr[s],
                      adaw_d[l, :, q * 512:(q + 1) * 512].rearrange("(k p) c -> p k c", p=128), writes=[r_wr[s]])
                if q < 4:
                    for jj in range(4):
                        j = q * 4 + jj
                        for kk in range(8):
                            k.op("pe", lambda e, s=s, jj=jj, j=j, kk=kk: e.matmul(
                                pm[:, j * 2:(j + 1) * 2], lhsT=wr[s][:, kk, jj * 128:(jj + 1) * 128],
                                rhs=scb[:, kk, :], start=(kk == 0), stop=(kk == 7)),
                                reads=[r_wr[s], r_scb], writes=[r_pm])
                else:
                    for c in range(2):
                        pg, r_pg = bank()
                        for kk in range(8):
                            k.op("pe", lambda e, s=s, c=c, kk=kk, pg=pg: e.matmul(
                                pg[:], lhsT=screp[:, c, kk, :], rhs=wr[s][:, kk, :],
                                start=(kk == 0), stop=(kk == 7)),
                                reads=[r_wr[s], r_screp], writes=[r_pg])
                        k.op("dve", lambda e, c=c, q=q, pg=pg: e.tensor_tensor(
                            out=GT[:, c, (q - 4) * 512:(q - 3) * 512], in0=pg[:],
                            in1=adabg[:, (q - 4) * 512:(q - 3) * 512], op=ALU.add),
                            reads=[r_pg, r_adabg], writes=[r_GT])
            lo = PK["ada_b"][0] + l * 24
            k.op("dve", lambda e: e.tensor_tensor(
                out=msb[:], in0=pm[:, 0:32].rearrange("p (j c) -> p j c", c=2),
                in1=pk[:, lo:lo + 16].unsqueeze(2).to_broadcast([128, 16, 2]), op=ALU.add),
                reads=[r_pm, r_pk], writes=[r_msb])
            k.op("dve", lambda e: e.tensor_copy(out=SHt[:], in_=msb[:, 0:8, :]), reads=[r_msb], writes=[r_G])
            glo = PK["norm_g"][0] + l * 8
            k.op("dve", lambda e: e.scalar_tensor_tensor(
                out=Gt[:], in0=msb[:, 8:16, :], scalar=1.0,
                in1=pk[:, glo:glo + 8].unsqueeze(2).to_broadcast([128, 8, 2]), op0=ALU.add, op1=ALU.mult),
                reads=[r_msb, r_pk], writes=[r_G])
            k.barrier()
            k.pop()

        junk = k.sb("junk", [128, 1024], BF16)
        r_junk = Res()

        def sumsq(x_ap, r_x, npart, ss_col, r_ss):
            k.op("act", lambda e: e.activation(out=junk[0:npart, :], in_=x_ap, func=AF.Square, accum_out=ss_col),
                 reads=[r_x], writes=[r_junk, r_ss])

        def rstd_of(ss_ap, rs_ap, npart, r_ss, scale):
            k.op("act", lambda e: e.activation(out=rs_ap, in_=ss_ap, func=AF.Ln, scale=scale, bias=epsc[0:npart, :]),
                 reads=[r_ss, r_eps], writes=[r_ss])
            k.op("act", lambda e: e.activation(out=rs_ap, in_=rs_ap, func=AF.Exp, scale=-0.5),
                 reads=[r_ss], writes=[r_ss])

        def xn_transpose(x_ap, r_x, npart, cond, hT_out_fn, r_hT, rs_col, r_ss, phase=None):
            if phase in (None, "a"):
                k.op("dve", lambda e: e.tensor_scalar(out=x_ap, in0=x_ap, scalar1=rs_col,
                                                      scalar2=0.0, op0=ALU.mult, op1=ALU.add),
                     reads=[r_x, r_ss], writes=[r_x])
            if phase == "a":
                return
            for half in range(2):
                pt, r_pt = bank()
                for kk in range(4):
                    kq = half * 4 + kk
                    k.op("pe", lambda e, kk=kk, kq=kq, pt=pt: e.transpose(
                        out=pt[:, kk * 128:kk * 128 + npart], in_=x_ap[:, kq * 128:(kq + 1) * 128],
                        identity=identf[0:npart, 0:npart]),
                        reads=[r_x, r_pk], writes=[r_pt])
                for kk in range(4):
                    kq = half * 4 + kk
                    k.op("act", lambda e, kk=kk, kq=kq, pt=pt: e.activation(
                        out=hT_out_fn(kq), in_=pt[:, kk * 128:kk * 128 + npart], func=AF.Identity,
                        scale=Gt[:, kq, cond:cond + 1], bias=SHt[:, kq, cond:cond + 1]),
                        reads=[r_pt, r_G], writes=[r_hT])

        epsc = k.sb("epsc", [128, 1])
        r_eps = Res()
        k.op("pool", lambda e: e.memset(epsc[:], EPS), writes=[r_eps])

        adaln(0)
        dump(10, GT[:, 0, :], 1024, [r_GT])
        dump(11, Gt[:].rearrange("p k c -> p (k c)"), 16, [r_G])
        stage(1)
        wout = k.sb("wout", [128, 16, 1024], BF16)
        r_wout = Res()
        k.dsem("wout")

        def load_wout(src):
            for q in range(4):
                k.dma("pool", "wout", wout[:, q * 4:(q + 1) * 4, :],
                      src[q * 512:(q + 1) * 512, :].rearrange("(e p) c -> p e c", p=128), writes=[r_wout])

        load_wout(cwout_d)

        k.push()
        w05 = k.sb("w05", [128, 16, 31])
        r_w05 = Res()
        k.op("dve", lambda e: e.tensor_scalar(out=w05[:].rearrange("p a b -> p (a b)"), in0=pkc("cw_dw"),
                                              scalar1=0.5, scalar2=0.0, op0=ALU.mult, op1=ALU.add),
             reads=[r_pk], writes=[r_w05])
        bg05 = k.sb("bg05", [128, 16])
        k.op("dve", lambda e: e.tensor_scalar(out=bg05[:], in0=pkc("cb_in", 16, 32), scalar1=0.5, scalar2=0.0,
                                              op0=ALU.mult, op1=ALU.add),
             reads=[r_pk], writes=[r_w05])

        xt = [k.sb("xt%d" % i, [128, 4, 1024]) for i in range(1)] * 2
        r_xt = [Res()] * 2
        k.dsem("xt0")
        xr = [k.sb("xr%d" % i, [128, 1024]) for i in range(2)]
        r_xr = [Res() for _ in range(2)]
        for i in range(2):
            k.dsem("xr%d" % i)
        ss0 = [k.sb("ss0_%d" % i, [128, 8]) for i in range(2)]
        rs0 = [k.sb("rs0_%d" % i, [128, 8]) for i in range(2)]
        r_ss0 = [Res() for _ in range(2)]
        hT = [k.sb("hT%d" % i, [128, 8, 512], BF16) for i in range(1)] * 2
        r_hT = [Res()] * 2
        a_t = [k.sb("a_t%d" % i, [128, 512]) for i in range(2)]
        th_t = [k.sb("th_t%d" % i, [128, 512]) for i in range(2)]
        r_a = [Res() for _ in range(2)]
        r_th = [Res() for _ in range(2)]
        vp = [k.sb("vp%d" % i, [128, 768], BF16) for i in range(2)]
        r_vp = [Res() for _ in range(2)]
        dg = [k.sb("dg%d" % i, [128, 31, 128], BF16) for i in range(2)]
        r_dg = [Res() for _ in range(2)]
        sqb = [k.sb("sqb%d" % i, [128, 512], BF16) for i in range(2)]
        r_sqb = [Res() for _ in range(2)]
        cv = k.sb("cv", [128, 16, 512], BF16)
        r_cv = [Res() for _ in range(16)]
        sz = k.sb("sz", [128, 16, 512], BF16)
        r_sz = [Res() for _ in range(16)]
        mean_t = k.sb("mean_t", [128, 512])
        rstd_t = k.sb("rstd_t", [128, 512])
        r_mean = Res()
        r_rstd = Res()
        t1 = [k.sb("t1_%d" % i, [128, 512]) for i in range(2)]
        r_t1 = [Res() for _ in range(2)]
        ub = [k.sb("ub%d" % i, [128, 512], BF16) for i in range(2)]
        r_ub = [Res() for _ in range(2)]
        yt = [k.sb("yt%d" % i, [128, 1024]) for i in range(2)]
        r_yt = [Res() for _ in range(2)]
        for i in range(2):
            store_sems.append(k.dsem("yt%d" % i))
        ytst = {"i": 0}

        NG0 = 5
        print("L0 sbuf bytes remaining", nc.sbuf_bytes_remaining)

        def l0_geom(g):
            return (8, 64, 0) if g < 4 else (2, 256, 1)

        def l0_xload(g):
            b = g % 2
            k.dma("sp", "xt0", xt[b][:], x_d[g * 512:(g + 1) * 512, :].rearrange("(t p) d -> p t d", p=128),
                  writes=[r_xt[b]])

        def l0_front(g):
            b = g % 2
            cond = l0_geom(g)[2]
            for tt in range(4):
                sumsq(xt[b][:, tt, :], r_xt[b], 128, ss0[b][:, tt:tt + 1], r_ss0[b])
            rstd_of(ss0[b][:, 0:4], rs0[b][:, 0:4], 128, r_ss0[b], 1.0 / D)
            for tt in range(4):
                xn_transpose(xt[b][:, tt, :], r_xt[b], 128, cond,
                             lambda kq, tt=tt, b=b: hT[b][:, kq, tt * 128:(tt + 1) * 128], r_hT[b],
                             rs0[b][:, tt:tt + 1], r_ss0[b])

        wr2x = k.sb("wr2x", [128, 8, 768], BF16)
        wr2.append(wr2x[:])
        r_wr2.append(Res())

        def l0_wload(pi):
            g, ep = pairs[pi]
            s = pi % 3
            for part in range(3):
                k.dma("pool", "wr%d" % s, wr2[s][:, :, part * 256:(part + 1) * 256],
                      cwin_d[:, part * 2048 + ep * 256: part * 2048 + (ep + 1) * 256].rearrange("(k p) c -> p k c", p=128),
                      writes=[r_wr2[s]])

        blocks = [(g, e) for g in range(NG0) for e in range(16)]
        pairs = [(g, ep) for g in range(NG0) for ep in range(8)]
        k.barrier()
        l0_wload(0)
        l0_wload(1)

        mean_ps, r_meanps = banks[0]
        ex2_ps, r_ex2ps = banks[1]

        stage(2)
        l0_xload(0)
        l0_front(0)
        stage(3)
        glo_ln = PK["cln_g"][0]
        bbo_ln = PK["cln_b"][0]
        gT = k.sb("gT", [128, 16, 512], BF16)
        r_gT = [Res() for _ in range(16)]
        pend_ln = [None]
        pend_op = [None]

        def ln_item(e2):
            p2 = e2 % 2
            k.op("pool", lambda e_: e_.tensor_tensor(out=t1[p2][:], in0=cv[:, e2, :], in1=mean_t[:], op=ALU.subtract),
                 reads=[r_cv[e2], r_mean], writes=[r_t1[p2]])
            k.op("dve", lambda e_: e_.tensor_tensor(out=t1[p2][:], in0=t1[p2][:], in1=rstd_t[:], op=ALU.mult),
                 reads=[r_t1[p2], r_rstd], writes=[r_t1[p2]])
            k.op("act", lambda e_: e_.activation(out=ub[p2][:], in_=t1[p2][:], func=AF.Silu,
                                                 scale=pk[:, glo_ln + e2:glo_ln + e2 + 1],
                                                 bias=pk[:, bbo_ln + e2:bbo_ln + e2 + 1]),
                 reads=[r_t1[p2], r_pk], writes=[r_ub[p2]])
            k.op("dve", lambda e_: e_.tensor_tensor(out=gT[:, e2, :], in0=sz[:, e2, :], in1=ub[p2][:], op=ALU.mult),
                 reads=[r_sz[e2], r_ub[p2]], writes=[r_gT[e2]])

        def outproj(g, cond):
            for tt in range(4):
                yi = ytst["i"] % 2
                ytst["i"] += 1
                row0 = g * 512 + tt * 128
                k.dma("sp", "xr%d" % yi, xr[yi][:], x_d[row0:row0 + 128, :], writes=[r_xr[yi]])
                for half in range(2):
                    po, r_po = bank()
                    for e2 in range(16):
                        k.op("pe", lambda e_, e2=e2, po=po: e_.matmul(
                            po[:], lhsT=gT[:, e2, tt * 128:(tt + 1) * 128],
                            rhs=wout[:, e2, half * 512:(half + 1) * 512], start=(e2 == 0), stop=(e2 == 15)),
                            reads=[r_gT[e2], r_wout], writes=[r_po])
                    k.op("dve", lambda e_, po=po: e_.tensor_tensor(
                        out=yt[yi][:, half * 512:(half + 1) * 512], in0=po[:],
                        in1=GT[:, cond, half * 512:(half + 1) * 512], op=ALU.mult),
                        reads=[r_po, r_GT], writes=[r_yt[yi]])
                k.op("pool", lambda e_: e_.tensor_tensor(out=yt[yi][:], in0=yt[yi][:], in1=xr[yi][:], op=ALU.add),
                     reads=[r_yt[yi], r_xr[yi]], writes=[r_yt[yi]])
                dst = y_d if stop_after_l0 else y1_d
                k.dma("sp", "yt%d" % yi, dst[row0:row0 + 128, :], yt[yi][:], reads=[r_yt[yi]])

        vp_layout = [None, None]
        pend0 = [None]
        r_dgs = [Res() for _ in range(16)]
        for nm in ("dgst", "dgld0", "dgld1"):
            k.dsem(nm)
        for bi, (g, e) in enumerate(blocks):
            b = g % 2
            par = bi % 2
            R, W, cond = l0_geom(g)
            Wp = W + 30
            s = (bi // 2) % 3
            if bi % 2 == 0 and bi // 2 + 2 < len(pairs):
                l0_wload(bi // 2 + 2)
            if g == 0:
                k.op("dve", lambda e_, par=par, e=e: e_.tensor_tensor(
                    out=dg[par][:], in0=identb[:].unsqueeze(1).to_broadcast([128, 31, 128]),
                    in1=w05[:, e, :].unsqueeze(2).to_broadcast([128, 31, 128]), op=ALU.mult),
                    reads=[r_idb, r_w05], writes=[r_dg[par]])
                k.dma("sp", "dgst", dgs_d[e], dg[par][:].rearrange("p k j -> p (k j)"), reads=[r_dg[par]],
                      writes=[r_dgs[e]])
            else:
                k.dma("sp", "dgld%d" % par, dg[par][:].rearrange("p k j -> p (k j)"), dgs_d[e], reads=[r_dgs[e]],
                      writes=[r_dg[par]])
            if vp_layout[par] != (R, W):
                k.op("pool", lambda e_, par=par: e_.memset(vp[par][:], 0.0), writes=[r_vp[par]])
                vp_layout[par] = (R, W)
            pa, r_pa = bank()
            pg, r_pg = bank()
            pz, r_pz = bank()
            for (pp, r_pp, part) in ((pa, r_pa, 0), (pg, r_pg, 1), (pz, r_pz, 2)):
                for kk in range(8):
                    c0 = part * 256 + (e % 2) * 128
                    k.op("pe", lambda e_, pp=pp, c0=c0, kk=kk, s=s, b=b: e_.matmul(
                        pp[:], lhsT=wr2[s][:, kk, c0:c0 + 128], rhs=hT[b][:, kk, :],
                        start=(kk == 0), stop=(kk == 7)),
                        reads=[r_wr2[s], r_hT[b]], writes=[r_pp])
            blo = PK["cb_in"][0]
            k.op("act", lambda e_, par=par, pa=pa, e=e: e_.activation(
                out=a_t[par][:], in_=pa[:], func=AF.Identity, bias=pk[:, blo + e:blo + e + 1]),
                reads=[r_pa, r_pk], writes=[r_a[par]])
            k.op("act", lambda e_, par=par, pg=pg, e=e: e_.activation(
                out=th_t[par][:], in_=pg[:], func=AF.Tanh, scale=0.5, bias=bg05[:, e:e + 1]),
                reads=[r_pg, r_w05], writes=[r_th[par]])
            vpv = vp[par][:, 0:R * Wp].rearrange("p (r w) -> p r w", w=Wp)
            k.op("dve", lambda e_, par=par, vpv=vpv, W=W: e_.scalar_tensor_tensor(
                out=vpv[:, :, 15:15 + W], in0=th_t[par][:].rearrange("p (r w) -> p r w", w=W), scalar=1.0,
                in1=a_t[par][:].rearrange("p (r w) -> p r w", w=W), op0=ALU.add, op1=ALU.mult),
                reads=[r_a[par], r_th[par]], writes=[r_vp[par]])
            if pend_ln[0] is not None:
                ln_item(e)
            k.op("act", lambda e_, pz=pz, e=e: e_.activation(
                out=sz[:, e, :], in_=pz[:], func=AF.Silu, bias=pk[:, blo + 32 + e:blo + 33 + e]),
                reads=[r_pz, r_pk], writes=[r_sz[e]])
            def conv_block(e, par, vpv, W):
                pc, r_pc = bank()
                pcv = pc[:].rearrange("p (r w) -> p r w", w=W)
                for t in range(31):
                    k.op("pe", lambda e_, t=t: e_.matmul(
                        pcv, lhsT=dg[par][:, t, :], rhs=vpv[:, :, t:t + W], start=(t == 0), stop=(t == 30)),
                        reads=[r_dg[par], r_vp[par]], writes=[r_pc])
                dlo = PK["cb_dw"][0]
                k.op("act", lambda e_: e_.activation(
                    out=cv[:, e, :], in_=pc[:], func=AF.Identity, bias=pk[:, dlo + e:dlo + e + 1]),
                    reads=[r_pc, r_pk], writes=[r_cv[e]])
            if pend0[0] is not None:
                conv_block(*pend0[0])
            pend0[0] = (e, par, vpv, W)
            if e == 15:
                conv_block(*pend0[0])
                pend0[0] = None
            if e == 8 and g + 1 < NG0:
                l0_xload(g + 1)
            if bi == 0:
                stage(4)
            if bi == 4:
                stage(5)
            if STAGE >= 100 and bi == STAGE - 100:
                stage(STAGE)
            if e == 15:
                for ep in range(16):
                    k.op("pe", lambda e_, ep=ep: e_.matmul(mean_ps[:], lhsT=onesE[:], rhs=cv[:, ep, :],
                                                           start=(ep == 0), stop=(ep == 15)),
                         reads=[r_onesE, r_cv[ep]], writes=[r_meanps])
                for ep in range(16):
                    pq = ep % 2
                    k.op("act", lambda e_, ep=ep, pq=pq: e_.activation(out=sqb[pq][:], in_=cv[:, ep, :], func=AF.Square),
                         reads=[r_cv[ep]], writes=[r_sqb[pq]])
                    k.op("pe", lambda e_, ep=ep, pq=pq: e_.matmul(ex2_ps[:], lhsT=onesE[:], rhs=sqb[pq][:],
                                                                  start=(ep == 0), stop=(ep == 15)),
                         reads=[r_onesE, r_sqb[pq]], writes=[r_ex2ps])
                if pend_op[0] is not None:
                    outproj(*pend_op[0])
                    pend_op[0] = None
                if g + 1 < NG0:
                    l0_front(g + 1)
                k.op("act", lambda e_: e_.activation(out=mean_t[:], in_=mean_ps[:], func=AF.Copy),
                     reads=[r_meanps], writes=[r_mean])
                k.op("pool", lambda e_: e_.tensor_tensor(out=rstd_t[:], in0=mean_t[:], in1=mean_t[:], op=ALU.mult),
                     reads=[r_mean], writes=[r_rstd])
                k.op("dve", lambda e_: e_.tensor_tensor(out=rstd_t[:], in0=ex2_ps[:], in1=rstd_t[:], op=ALU.subtract),
                     reads=[r_ex2ps, r_rstd], writes=[r_rstd])
                k.op("act", lambda e_: e_.activation(out=rstd_t[:], in_=rstd_t[:], func=AF.Ln, bias=epsc[:, :]),
                     reads=[r_rstd, r_eps], writes=[r_rstd])
                k.op("act", lambda e_: e_.activation(out=rstd_t[:], in_=rstd_t[:], func=AF.Exp, scale=-0.5),
                     reads=[r_rstd], writes=[r_rstd])
                if pend_op[0] is not None:
                    outproj(*pend_op[0])
                    pend_op[0] = None
                pend_ln[0] = g
                pend_op[0] = (g, cond)
                if g == NG0 - 1:
                    for e2 in range(16):
                        ln_item(e2)
                    outproj(g, cond)
                    pend_op[0] = None
                    pend_ln[0] = None

        if stop_after_l0:
            k.finish(store_sems)
            k.pop()
            print("instructions", k.nins, "waits", k.nwait)
            return nc
        k.barrier()
        k.pop()

        k.push()
        V = k.op
        adaln(1)
        rotcfg["base"] = 0
        rotcfg["n"] = 8
        r_c = Res()

        def cmat(name, pattern, cm, base):
            t = k.sb(name, [128, 128])
            V("pool", lambda e: e.memset(t[:], 1.0), writes=[r_c])
            if pattern is not None:
                V("pool", lambda e: e.affine_select(out=t[:], in_=t[:], pattern=pattern, compare_op=ALU.is_ge,
                                                    fill=0.0, base=base, channel_multiplier=cm), reads=[r_c], writes=[r_c])
            return t

        TRIle = cmat("TRIle", [[1, 128]], -1, 0)
        TRIge = cmat("TRIge", [[-1, 128]], 1, 0)
        UFf = cmat("UFf", [[-1, 128]], 1, -1)
        UBf = cmat("UBf", [[1, 128]], -1, -1)
        ONESf = cmat("ONESf", None, 0, 0)

        def tobf(name, src):
            t = k.sb(name, [128, 128], BF16)
            V("dve", lambda e: e.tensor_copy(out=t[:], in_=src[:]), reads=[r_c], writes=[r_c])
            return t

        TRIle_b = tobf("TRIle_b", TRIle)
        TRIge_b = tobf("TRIge_b", TRIge)
        UF_b = tobf("UF_b", UFf)
        UB_b = tobf("UB_b", UBf)
        one_c = k.sb("one_c", [128, 1])
        V("pool", lambda e: e.memset(one_c[:], 1.0), writes=[r_c])
        aneg = k.sb("aneg", [128, 64])
        V("act", lambda e: e.activation(out=aneg[:], in_=pkc("alog"), func=AF.Exp), reads=[r_pk], writes=[r_c])
        V("dve", lambda e: e.tensor_scalar(out=aneg[:], in0=aneg[:], scalar1=-1.0, scalar2=0.0, op0=ALU.mult, op1=ALU.add),
          reads=[r_c], writes=[r_c])

        sng = k.sb("sng", [128, 2048], BF16)
        fng = k.sb("fng", [128, 1024])
        wdt = k.sb("wdt", [128, 8, 64], BF16)
        for nm in ("sng", "fng", "wdt", "xw", "xh", "prevb", "spill", "y2", "stout", "xres"):
            k.dsem(nm)
        store_sems.extend(["y2", "stout", "spill"])
        k.dma("pool", "sng", sng[:], sng_d[:, :], writes=[r_c])
        k.dma("sp", "fng", fng[:], fng_d[:, :], writes=[r_c])
        k.dma("pool", "wdt", wdt[:], swin_d[:, 5120:5184].rearrange("(k p) c -> p k c", p=128), writes=[r_c])

        xw = k.sb("xw", [128, 3, 1024]); r_xw = Res()
        xh = xw[:, 2, :]; r_xh = r_xw
        y2 = k.sb("y2", [128, 1024]); r_y2 = Res()
        xres = k.sb("xres", [128, 1024]); r_xres = Res()
        ss1 = k.sb("ss1", [128, 4]); rs1 = k.sb("rs1", [128, 4]); r_ss1 = Res()
        V("pool", lambda e: e.memset(ss1[:], 1.0), writes=[r_ss1])
        hTw2 = [k.sb("hTw%d" % i, [128, 8, 260], BF16) for i in range(2)]; r_hTw2 = [Res(), Res()]
        ub = [k.sb("ub1_%d" % i, [128, 516], BF16) for i in range(2)]; r_ub = [Res(), Res()]
        dg1 = [k.sb("dg1_%d" % i, [128, 5, 128], BF16) for i in range(2)]; r_dg1 = [Res(), Res()]
        xbcT2 = [k.sb("xbcT%d" % i, [128, 24, 256], BF16) for i in range(2)]
        r_xbc2 = [[Res() for _ in range(24)] for _ in range(2)]
        szTc = [k.sb("szT%d" % i, [128, 2048], BF16) for i in range(2)]; r_szTc = [Res(), Res()]
        for nm in ("spx", "spz0", "spz1", "spd", "ldx0", "ldx1", "ldz0", "ldz1", "ldd0", "ldd1"):
            k.dsem(nm)
        store_sems.extend(["spx", "spz0", "spz1", "spd"])
        r_xsp = [Res() for _ in range(10)]; r_zsp = [[Res(), Res()] for _ in range(10)]; r_dsp = [Res() for _ in range(10)]
        dtt2 = [k.sb("dtt%d" % i, [128, 2, 64]) for i in range(2)]; r_dtt2 = [Res(), Res()]
        S_one = k.sb("S_one", [128, 2048]); r_S_one = Res()
        S = [S_one, S_one]; r_S = [r_S_one, r_S_one]
        Sbf = k.sb("Sbf", [128, 2048], BF16); r_Sbf = Res()
        prevb = k.sb("prevb", [128, 2048], BF16); r_prevb = Res()
        r_sbs = [Res() for _ in range(20)]
        xtok = k.sb("xtok", [128, 2048], BF16); r_xtok = Res()
        btok = k.sb("btok", [128, 512], BF16); r_btok = Res()
        la = [k.sb("la%d" % d, [128, 32]) for d in range(2)]
        acs = [k.sb("acs%d" % d, [128, 32]) for d in range(2)]
        ea = [k.sb("ea%d" % d, [128, 32]) for d in range(2)]
        cd = [k.sb("cd%d" % d, [128, 32]) for d in range(2)]
        dte = [k.sb("dte%d" % d, [128, 32]) for d in range(2)]
        wd = [k.sb("wd%d" % d, [128, 32]) for d in range(2)]
        r_dq = [Res(), Res()]
        cbm = [k.sb("cbm%d" % d, [128, 4, 128], BF16) for d in range(2)]; r_cbm = Res()
        Rt = [k.sb("Rt%d" % d, [128, 8, 128], BF16) for d in range(2)]; r_Rt = [Res(), Res()]
        Eg = [[k.sb("Eg%d_%d" % (q, d), [128, 8, 128], BF16) for d in range(2)] for q in range(2)]
        r_Eg = [[Res(), Res()], [Res(), Res()]]
        xdt = [[k.sb("xdt%d_%d" % (q, d), [128, 512], BF16) for d in range(2)] for q in range(2)]
        r_xdt = [[Res(), Res()], [Res(), Res()]]
        xdteB = [k.sb("xdteB%d" % q, [128, 512], BF16) for q in range(2)]; r_xdteB = [Res(), Res()]
        xdte = k.sb("xdte", [128, 512], BF16); r_xdte = Res()
        u1 = k.sb("u1", [128, 512]); u2 = k.sb("u2", [128, 512]); yg = k.sb("yg", [128, 512])
        t3 = [k.sb("t3_%d" % q, [128, 512]) for q in range(2)]
        r_u1 = Res(); r_u2 = Res(); r_t3 = [Res(), Res()]; r_yg = Res()
        ssg = k.sb("ssg", [128, 2]); r_ssg = Res()
        yn = [k.sb("yn%d" % q, [128, 512], BF16) for q in range(2)]; r_yn = [Res(), Res()]
        ynT = k.sb("ynT", [128, 16, 128], BF16); r_ynT = Res()
        tS = k.sb("tS", [128, 512]); r_tS = Res()
        print("L1 sbuf bytes remaining", nc.sbuf_bytes_remaining)

        def bc8(ap32, g):
            return ap32[:, g * 8:(g + 1) * 8].unsqueeze(2).to_broadcast([128, 8, 64])

        def v864(ap512):
            return ap512.rearrange("p (h q) -> p h q", q=64)

        def wstream(srcs):
            st = {"n": 0, "slot": {}}

            def get(i):
                while st["n"] < len(srcs) and st["n"] <= i + 2:
                    sl = wslot()
                    k.dma("pool", "wr%d" % sl, wr[sl], srcs[st["n"]], writes=[r_wr[sl]])
                    st["slot"][st["n"]] = sl
                    st["n"] += 1
                return st["slot"][i]
            return get

        def wsrc(c0):
            return swin_d[:, c0:c0 + 512].rearrange("(k p) c -> p k c", p=128)

        def l1_front(base, nw, w, cond, wb, phase=None):
            hTw = hTw2[wb]; r_hTw = r_hTw2[wb]
            t0 = base + w * 256
            if phase in (None, "a", "a0"):
                k.dma("sp", "xw", xw[:, 0:2, :], y1_d[t0:t0 + 256, :].rearrange("(t p) d -> p t d", p=128), writes=[r_xw])
                V("pool", lambda e: e.memset(xh[0:4, :], 0.0), writes=[r_xh])
                if w > 0:
                    k.dma("sp", "xh", xh[0:2, :], y1_d[t0 - 2:t0, :], writes=[r_xh])
                if w < nw - 1:
                    k.dma("sp", "xh", xh[2:4, :], y1_d[t0 + 256:t0 + 258, :], writes=[r_xh])
            if phase == "a0":
                return
            if phase in (None, "a", "a1"):
                sumsq(xw[:, 0, :], r_xw, 128, ss1[:, 0:1], r_ss1)
                sumsq(xw[:, 1, :], r_xw, 128, ss1[:, 1:2], r_ss1)
                sumsq(xh[0:4, :], r_xh, 4, ss1[0:4, 2:3], r_ss1)
                rstd_of(ss1[:, 0:3], rs1[:, 0:3], 128, r_ss1, 1.0 / D)
            ph2 = "a" if phase == "a1" else phase
            xn_transpose(xw[:, 0, :], r_xw, 128, cond, lambda kq: hTw[:, kq, 0:128], r_hTw, rs1[:, 0:1], r_ss1, ph2)
            xn_transpose(xw[:, 1, :], r_xw, 128, cond, lambda kq: hTw[:, kq, 128:256], r_hTw, rs1[:, 1:2], r_ss1, ph2)
            xn_transpose(xh[0:4, :], r_xh, 4, cond, lambda kq: hTw[:, kq, 256:260], r_hTw, rs1[0:4, 2:3], r_ss1, ph2)
            if ph2 == "a":
                return
            if w == 0:
                V("pool", lambda e: e.memset(hTw[:, :, 256:258], 0.0), writes=[r_hTw])
            if w == nw - 1:
                V("pool", lambda e: e.memset(hTw[:, :, 258:260], 0.0), writes=[r_hTw])

        wclo = PK["sw_conv"][0]
        bclo = PK["sb_conv"][0]

        def l1_xbc_pieces(get, idxs, npieces, wb, only=None):
            hTw = hTw2[wb]; r_hTw = r_hTw2[wb]; xbcT = xbcT2[wb]; r_xbc = r_xbc2[wb]
            def conv_part(j, pj):
                pc, r_pc = bank()
                for t in range(5):
                    V("pe", lambda e, t=t, pj=pj, pc=pc: e.matmul(
                        pc[:, 0:256], lhsT=dg1[pj][:, t, :], rhs=ub[pj][:, t:t + 256], start=(t == 0), stop=(t == 4)),
                      reads=[r_dg1[pj], r_ub[pj]], writes=[r_pc])
                V("act", lambda e, j=j, pc=pc: e.activation(out=xbcT[:, j, :], in_=pc[:, 0:256], func=AF.Silu,
                                                            bias=pk[:, bclo + j:bclo + j + 1]),
                  reads=[r_pc, r_pk], writes=[r_xbc[j]])

            st = {"pend": None}

            def piece(pc_i):
                sl = get(idxs[pc_i])
                for jj in range(4):
                    j = pc_i * 4 + jj
                    pj = j % 2
                    V("dve", lambda e, pj=pj, j=j: e.tensor_tensor(
                        out=dg1[pj][:], in0=identb[:].unsqueeze(1).to_broadcast([128, 5, 128]),
                        in1=pk[:, wclo + j * 5:wclo + j * 5 + 5].unsqueeze(2).to_broadcast([128, 5, 128]), op=ALU.mult),
                      reads=[r_idb, r_pk], writes=[r_dg1[pj]])
                    pb, r_pb = bank()
                    for kk in range(8):
                        V("pe", lambda e, kk=kk, sl=sl, jj=jj, pb=pb: e.matmul(
                            pb[:, 0:260], lhsT=wr[sl][:, kk, jj * 128:(jj + 1) * 128], rhs=hTw[:, kk, :],
                            start=(kk == 0), stop=(kk == 7)), reads=[r_wr[sl], r_hTw], writes=[r_pb])
                    V("act", lambda e, pj=pj, pb=pb: e.activation(out=ub[pj][:, 2:258], in_=pb[:, 0:256], func=AF.Copy),
                      reads=[r_pb], writes=[r_ub[pj]])
                    V("act", lambda e, pj=pj, pb=pb: e.activation(
                        out=ub[pj][:, :].rearrange("p (a b) -> p a b", b=258)[:, :, 0:2],
                        in_=pb[:, 256:260].rearrange("p (a b) -> p a b", b=2), func=AF.Copy),
                      reads=[r_pb], writes=[r_ub[pj]])
                    if st["pend"] is not None:
                        conv_part(*st["pend"])
                    st["pend"] = (j, pj)
                if pc_i == npieces - 1 or only is not None:
                    conv_part(*st["pend"])
                    st["pend"] = None

            return [(lambda pc_i=pc_i: piece(pc_i)) for pc_i in range(npieces)]

        def l1_z_part(get, idx, wb, q):
            hTw = hTw2[wb]; r_hTw = r_hTw2[wb]
            sl = get(idx)
            for cc in range(2):
                pb, r_pb = bank()
                for kk in range(8):
                    V("pe", lambda e, kk=kk, cc=cc, pb=pb: e.matmul(
                        pb[:], lhsT=hTw[:, kk, cc * 128:(cc + 1) * 128], rhs=wr[sl][:, kk, :],
                        start=(kk == 0), stop=(kk == 7)), reads=[r_wr[sl], r_hTw], writes=[r_pb])
                V("act", lambda e, cc=cc, pb=pb: e.activation(
                    out=szTc[cc][:, q * 512:(q + 1) * 512], in_=pb[:], func=AF.Silu), reads=[r_pb], writes=[r_szTc[cc]])

        def l1_dt(wb):
            hTw = hTw2[wb]; r_hTw = r_hTw2[wb]; dtt = dtt2[wb]; r_dtt = r_dtt2[wb]
            for cc in range(2):
                pb, r_pb = bank()
                for kk in range(8):
                    V("pe", lambda e, kk=kk, cc=cc, pb=pb: e.matmul(
                        pb[:, 0:64], lhsT=hTw[:, kk, cc * 128:(cc + 1) * 128], rhs=wdt[:, kk, :],
                        start=(kk == 0), stop=(kk == 7)), reads=[r_c, r_hTw], writes=[r_pb])
                V("dve", lambda e, cc=cc, pb=pb: e.tensor_tensor(out=dtt[:, cc, :], in0=pb[:, 0:64], in1=pkc("dtb"), op=ALU.add),
                  reads=[r_pb, r_pk], writes=[r_dtt])
            dv = dtt[:].rearrange("p c h -> p (c h)")
            V("act", lambda e: e.activation(out=dv, in_=dv, func=AF.Exp), reads=[r_dtt], writes=[r_dtt])
            V("act", lambda e: e.activation(out=dv, in_=dv, func=AF.Ln, bias=one_c[:, :]), reads=[r_dtt, r_c], writes=[r_dtt])

        def tok_major(cc, wb):
            xbcT = xbcT2[wb]; r_xbc = r_xbc2[wb]
            c0 = cc * 128
            for q4 in range(4):
                pb, r_pb = bank()
                for i in range(4):
                    j = q4 * 4 + i
                    V("pe", lambda e, i=i, j=j, pb=pb: e.matmul(
                        pb[:, i * 128:(i + 1) * 128], lhsT=xbcT[:, j, c0:c0 + 128], rhs=identb[:], start=True, stop=True),
                      reads=[r_xbc[j], r_idb], writes=[r_pb])
                V("act", lambda e, q4=q4, pb=pb: e.activation(out=xtok[:, q4 * 512:(q4 + 1) * 512], in_=pb[:], func=AF.Copy),
                  reads=[r_pb], writes=[r_xtok])
            pb, r_pb = bank()
            for g in range(4):
                V("pe", lambda e, g=g, pb=pb: e.matmul(
                    pb[:, g * 128:(g + 1) * 128], lhsT=xbcT[:, 16 + g, c0:c0 + 128], rhs=identb[:], start=True, stop=True),
                  reads=[r_xbc[16 + g], r_idb], writes=[r_pb])
            V("dve", lambda e, pb=pb: e.tensor_copy(out=btok[:], in_=pb[:]), reads=[r_pb], writes=[r_btok])

        def dtq(cc, d, wb):
            dtt = dtt2[wb]; r_dtt = r_dtt2[wb]
            dtd = dtt[:, cc, d * 32:(d + 1) * 32]
            V("dve", lambda e: e.tensor_tensor(out=la[d][:], in0=dtd, in1=aneg[:, d * 32:(d + 1) * 32], op=ALU.mult),
              reads=[r_dtt, r_c], writes=[r_dq[d]])
            pb, r_pb = bank()
            V("pe", lambda e: e.matmul(pb[:, 0:32], lhsT=(TRIle if d == 0 else TRIge)[:], rhs=la[d][:], start=True, stop=True),
              reads=[r_c, r_dq[d]], writes=[r_pb])
            V("pe", lambda e: e.matmul(pb[:, 32:64], lhsT=ONESf[:], rhs=la[d][:], start=True, stop=True),
              reads=[r_c, r_dq[d]], writes=[r_pb])
            V("act", lambda e: e.activation(out=acs[d][:], in_=pb[:, 0:32], func=AF.Identity), reads=[r_pb], writes=[r_dq[d]])
            V("act", lambda e: e.activation(out=ea[d][:], in_=pb[:, 0:32], func=AF.Exp), reads=[r_pb], writes=[r_dq[d]])
            V("act", lambda e: e.activation(out=cd[d][:], in_=pb[:, 32:64], func=AF.Exp), reads=[r_pb], writes=[r_dq[d]])
            V("dve", lambda e: e.tensor_tensor(out=dte[d][:], in0=pb[:, 32:64], in1=acs[d][:], op=ALU.subtract),
              reads=[r_pb, r_dq[d]], writes=[r_dq[d]])
            V("act", lambda e: e.activation(out=dte[d][:], in_=dte[d][:], func=AF.Exp), reads=[r_dq[d]], writes=[r_dq[d]])
            V("dve", lambda e: e.tensor_tensor(out=wd[d][:], in0=dtd, in1=dte[d][:], op=ALU.mult),
              reads=[r_dtt, r_dq[d]], writes=[r_dq[d]])

        def state_update(d, g):
            V("dve", lambda e: e.tensor_tensor(out=v864(xdte[:]), in0=v864(xtok[:, g * 512:(g + 1) * 512]),
                                                in1=bc8(wd[d], g), op=ALU.mult),
              reads=[r_xtok, r_dq[d]], writes=[r_xdte])
            pb, r_pb = bank()
            V("pe", lambda e: e.matmul(pb[:], lhsT=btok[:, g * 128:(g + 1) * 128], rhs=xdte[:], start=True, stop=True),
              reads=[r_btok, r_xdte], writes=[r_pb])
            V("dve", lambda e: e.tensor_tensor(out=v864(tS[:]), in0=v864(S[d][:, g * 512:(g + 1) * 512]),
                                               in1=bc8(cd[d], g), op=ALU.mult),
              reads=[r_S[d], r_dq[d]], writes=[r_tS])
            V("dve", lambda e: e.tensor_tensor(out=S[d][:, g * 512:(g + 1) * 512], in0=pb[:], in1=tS[:], op=ALU.add),
              reads=[r_pb, r_tS], writes=[r_S[d]])

        stv = xw[:, 0:2, :].rearrange("p t (j n) -> p (t j) n", n=128)

        def state_load(d, src):
            k.dma("sp", "xw", stv, src.rearrange("(j p) n -> p j n", p=128), writes=[r_xw])
            for q4 in range(4):
                pb, r_pb = bank()
                for i in range(4):
                    j = q4 * 4 + i
                    V("pe", lambda e, i=i, j=j, pb=pb: e.transpose(out=pb[:, i * 128:(i + 1) * 128], in_=stv[:, j, :],
                                                                   identity=identf),
                      reads=[r_xw, r_pk], writes=[r_pb])
                V("act", lambda e, q4=q4, pb=pb: e.activation(out=S[d][:, q4 * 512:(q4 + 1) * 512], in_=pb[:], func=AF.Copy),
                  reads=[r_pb], writes=[r_S[d]])

        def state_store(d, dst):
            for q4 in range(4):
                pb, r_pb = bank()
                for i in range(4):
                    j = q4 * 4 + i
                    V("pe", lambda e, i=i, j=j, pb=pb: e.transpose(out=pb[:, i * 128:(i + 1) * 128],
                                                                   in_=S[d][:, j * 128:(j + 1) * 128], identity=identf),
                      reads=[r_S[d], r_pk], writes=[r_pb])
                V("act", lambda e, q4=q4, pb=pb: e.activation(
                    out=stv[:, q4 * 4:(q4 + 1) * 4, :], in_=pb[:].rearrange("p (j n) -> p j n", n=128), func=AF.Copy),
                  reads=[r_pb], writes=[r_xw])
            k.dma("sp", "stout", dst.rearrange("(j p) n -> p j n", p=128), stv, reads=[r_xw])

        def chunkA(cc, cid, wb, slots=()):
            slots = list(slots)
            V("act", lambda e: e.activation(out=Sbf[:], in_=S[1][:], func=AF.Copy), reads=[r_S[1]], writes=[r_Sbf])
            k.dma("sp", "spill", sbs_d[cid], Sbf[:], reads=[r_Sbf], writes=[r_sbs[cid]])
            dtq(cc, 1, wb)
            tok_major(cc, wb)
            if slots:
                slots.pop(0)()
            for g in range(4):
                state_update(1, g)
                if g in (0, 2) and slots:
                    slots.pop(0)()
            while slots:
                slots.pop(0)()

        nglo = 0

        def chunkB(cc, cid, row0, cond, wb, pre_fns, mid_fns):
            c0 = cc * 128
            xbcT = xbcT2[wb]; r_xbc = r_xbc2[wb]; dtt = dtt2[wb]; r_dtt = r_dtt2[wb]
            k.dma("sp", "xres", xres[:], y1_d[row0:row0 + 128, :], writes=[r_xres])
            k.dma("sp", "prevb", prevb[:], sbs_d[cid], reads=[r_sbs[cid]], writes=[r_prevb])
            V("act", lambda e: e.activation(out=Sbf[:], in_=S[0][:], func=AF.Copy), reads=[r_S[0]], writes=[r_Sbf])
            dtq(cc, 0, wb)
            dtq(cc, 1, wb)
            tok_major(cc, wb)
            pcb, r_pcb = bank()
            for g in range(4):
                V("pe", lambda e, g=g: e.matmul(pcb[:, g * 128:(g + 1) * 128], lhsT=xbcT[:, 16 + g, c0:c0 + 128],
                                                rhs=xbcT[:, 20 + g, c0:c0 + 128], start=True, stop=True),
                  reads=[r_xbc[16 + g], r_xbc[20 + g]], writes=[r_pcb])
            for d, M in ((0, TRIle_b), (1, TRIge_b)):
                V("dve", lambda e, d=d, M=M: e.tensor_tensor(
                    out=cbm[d][:], in0=pcb[:].rearrange("p (g l) -> p g l", l=128),
                    in1=M[:].unsqueeze(1).to_broadcast([128, 4, 128]), op=ALU.mult),
                  reads=[r_pcb, r_c], writes=[r_cbm])
            def S1(g):
                pq = g % 2
                for d in range(2):
                    Mb = TRIle_b if d == 0 else TRIge_b
                    Ub = UF_b if d == 0 else UB_b
                    V("pool", lambda e, d=d, Mb=Mb: e.tensor_tensor(
                        out=Rt[d][:], in0=Mb[:].unsqueeze(1).to_broadcast([128, 8, 128]),
                        in1=la[d][:, g * 8:(g + 1) * 8].unsqueeze(2).to_broadcast([128, 8, 128]), op=ALU.mult),
                      reads=[r_c, r_dq[d]], writes=[r_Rt[d]])
                    for hh in range(2):
                        pb, r_pb = bank()
                        V("pe", lambda e, d=d, hh=hh, pb=pb, Ub=Ub: e.matmul(
                            pb[:], lhsT=Ub[:], rhs=Rt[d][:, hh * 4:(hh + 1) * 4, :].rearrange("p h l -> p (h l)"),
                            start=True, stop=True), reads=[r_c, r_Rt[d]], writes=[r_pb])
                        V("act", lambda e, d=d, hh=hh, pb=pb: e.activation(
                            out=Eg[pq][d][:, hh * 4:(hh + 1) * 4, :].rearrange("p h l -> p (h l)"), in_=pb[:], func=AF.Exp),
                          reads=[r_pb], writes=[r_Eg[pq][d]])
                    V("dve", lambda e, d=d: e.tensor_tensor(
                        out=Eg[pq][d][:], in0=Eg[pq][d][:], in1=cbm[d][:, g, :].unsqueeze(1).to_broadcast([128, 8, 128]),
                        op=ALU.mult), reads=[r_Eg[pq][d], r_cbm], writes=[r_Eg[pq][d]])
                    V("dve" if d == 0 else "pool", lambda e, d=d: e.tensor_tensor(
                        out=v864(xdt[pq][d][:]), in0=v864(xtok[:, g * 512:(g + 1) * 512]),
                        in1=bc8(dtt[:, cc, d * 32:(d + 1) * 32], g), op=ALU.mult),
                      reads=[r_xtok, r_dtt], writes=[r_xdt[pq][d]])
                V("pool", lambda e: e.tensor_tensor(out=v864(t3[pq][:]), in0=v864(xtok[:, g * 512:(g + 1) * 512]),
                                                    in1=bc8(pkc("sd"), g), op=ALU.mult),
                  reads=[r_xtok, r_pk], writes=[r_t3[pq]])

            def S2a(g):
                pq = g % 2
                if pre_fns.get(g) is not None:
                    pre_fns[g]()
                pyd, r_pyd = bank()
                for h in range(8):
                    for d in range(2):
                        V("pe", lambda e, h=h, d=d: e.matmul(
                            pyd[:, h * 64:(h + 1) * 64], lhsT=Eg[pq][d][:, h, :], rhs=xdt[pq][d][:, h * 64:(h + 1) * 64],
                            start=(d == 0), stop=(d == 1)), reads=[r_Eg[pq][d], r_xdt[pq][d]], writes=[r_pyd])
                pof, r_pof = bank()
                V("pe", lambda e: e.matmul(pof[:], lhsT=xbcT[:, 20 + g, c0:c0 + 128], rhs=Sbf[:, g * 512:(g + 1) * 512],
                                           start=True, stop=True), reads=[r_xbc[20 + g], r_Sbf], writes=[r_pof])
                pob, r_pob = bank()
                V("pe", lambda e: e.matmul(pob[:], lhsT=xbcT[:, 20 + g, c0:c0 + 128], rhs=prevb[:, g * 512:(g + 1) * 512],
                                           start=True, stop=True), reads=[r_xbc[20 + g], r_prevb], writes=[r_pob])
                V("dve", lambda e: e.tensor_tensor(out=v864(u1[:]), in0=v864(pof[:]), in1=bc8(ea[0], g), op=ALU.mult),
                  reads=[r_pof, r_dq[0]], writes=[r_u1])
                V("dve", lambda e: e.tensor_tensor(out=v864(u2[:]), in0=v864(pob[:]), in1=bc8(ea[1], g), op=ALU.mult),
                  reads=[r_pob, r_dq[1]], writes=[r_u2])
                V("dve", lambda e: e.tensor_tensor(out=u1[:], in0=u1[:], in1=u2[:], op=ALU.add),
                  reads=[r_u1, r_u2], writes=[r_u1])
                V("dve", lambda e: e.tensor_tensor(out=u1[:], in0=u1[:], in1=t3[pq][:], op=ALU.add),
                  reads=[r_u1, r_t3[pq]], writes=[r_u1])
                V("pool", lambda e: e.tensor_tensor(out=v864(xdteB[pq][:]), in0=v864(xtok[:, g * 512:(g + 1) * 512]),
                                                    in1=bc8(wd[0], g), op=ALU.mult),
                  reads=[r_xtok, r_dq[0]], writes=[r_xdteB[pq]])
                V("dve", lambda e: e.tensor_tensor(out=yg[:], in0=pyd[:], in1=u1[:], op=ALU.add),
                  reads=[r_pyd, r_u1], writes=[r_yg])
                V("dve", lambda e: e.tensor_tensor(out=yg[:], in0=yg[:], in1=szTc[cc][:, g * 512:(g + 1) * 512], op=ALU.mult),
                  reads=[r_yg, r_szTc[cc]], writes=[r_yg])
                if mid_fns.get(g) is not None:
                    mid_fns[g]()
                V("act", lambda e: e.activation(out=junk[:, 0:512], in_=yg[:], func=AF.Square, accum_out=ssg[:, 0:1]),
                  reads=[r_yg], writes=[r_junk, r_ssg])
                rstd_of(ssg[:, 0:1], ssg[:, 1:2], 128, r_ssg, 1.0 / 512)
                V("dve", lambda e: e.scalar_tensor_tensor(
                    out=yn[pq][:], in0=yg[:], scalar=ssg[:, 1:2], in1=sng[:, g * 512:(g + 1) * 512],
                    op0=ALU.mult, op1=ALU.mult), reads=[r_yg, r_ssg, r_c], writes=[r_yn[pq]])

            def S2b(g):
                pq = g % 2
                pb, r_pb = bank()
                for i in range(4):
                    V("pe", lambda e, i=i, pb=pb: e.matmul(pb[:, i * 128:(i + 1) * 128], lhsT=yn[pq][:, i * 128:(i + 1) * 128],
                                                            rhs=identb[:], start=True, stop=True),
                      reads=[r_yn[pq], r_idb], writes=[r_pb])
                V("act", lambda e, pb=pb: e.activation(
                    out=ynT[:, g * 4:(g + 1) * 4, :], in_=pb[:].rearrange("p (j t) -> p j t", t=128), func=AF.Copy),
                  reads=[r_pb], writes=[r_ynT])
                ps_, r_ps = bank()
                V("pe", lambda e: e.matmul(ps_[:], lhsT=btok[:, g * 128:(g + 1) * 128], rhs=xdteB[pq][:], start=True, stop=True),
                  reads=[r_btok, r_xdteB[pq]], writes=[r_ps])
                V("pool", lambda e: e.tensor_tensor(out=v864(tS[:]), in0=v864(S[0][:, g * 512:(g + 1) * 512]),
                                                    in1=bc8(cd[0], g), op=ALU.mult),
                  reads=[r_S[0], r_dq[0]], writes=[r_tS])
                V("dve", lambda e: e.tensor_tensor(out=S[0][:, g * 512:(g + 1) * 512], in0=ps_[:], in1=tS[:], op=ALU.add),
                  reads=[r_ps, r_tS], writes=[r_S[0]])

            for step in ("1:0", "1:1", "a:0", "1:2", "a:1", "b:0", "1:3", "a:2", "b:1", "a:3", "b:2", "b:3"):
                kind, gi = step.split(":")
                {"1": S1, "a": S2a, "b": S2b}[kind](int(gi))
            for half in range(2):
                po, r_po = bank()
                for e2 in range(16):
                    V("pe", lambda e, e2=e2, half=half, po=po: e.matmul(
                        po[:], lhsT=ynT[:, e2, :], rhs=wout[:, e2, half * 512:(half + 1) * 512],
                        start=(e2 == 0), stop=(e2 == 15)), reads=[r_ynT, r_wout], writes=[r_po])
                V("dve", lambda e, half=half, po=po: e.tensor_tensor(
                    out=y2[:, half * 512:(half + 1) * 512], in0=po[:], in1=GT[:, cond, half * 512:(half + 1) * 512], op=ALU.mult),
                  reads=[r_po, r_GT], writes=[r_y2])
            V("pool", lambda e: e.tensor_tensor(out=y2[:], in0=y2[:], in1=xres[:], op=ALU.add),
              reads=[r_y2, r_xres], writes=[r_y2])
            V("act", lambda e: e.activation(out=junk[:], in_=y2[:], func=AF.Square, accum_out=ssg[:, 0:1]),
              reads=[r_y2], writes=[r_junk, r_ssg])
            rstd_of(ssg[:, 0:1], ssg[:, 1:2], 128, r_ssg, 1.0 / D)
            V("dve", lambda e: e.scalar_tensor_tensor(out=y2[:], in0=y2[:], scalar=ssg[:, 1:2], in1=fng[:],
                                                      op0=ALU.mult, op1=ALU.mult),
              reads=[r_y2, r_ssg, r_c], writes=[r_y2])
            k.dma("sp", "y2", y_d[row0:row0 + 128, :], y2[:], reads=[r_y2])

        seqs = [(0, 8, 0, True, None, 0), (2048, 1, 1, False, 0, 16), (2304, 1, 1, False, 1, 18)]
        if STAGE >= 200:
            seqs = seqs[STAGE - 200:STAGE - 199]
        def spill_window(gw, wb):
            k.dma("sp", "spx", xsp_d[gw], xbcT2[wb][:].rearrange("p j t -> p (j t)"), reads=r_xbc2[wb], writes=[r_xsp[gw]])
            for cc in range(2):
                k.dma("sp", "spz%d" % cc, zsp_d[gw, cc], szTc[cc][:], reads=[r_szTc[cc]], writes=[r_zsp[gw][cc]])
            k.dma("sp", "spd", dsp_d[gw], dtt2[wb][:].rearrange("p c h -> p (c h)"), reads=[r_dtt2[wb]], writes=[r_dsp[gw]])

        def load_xd(gw, wb):
            k.dma("sp", "ldx%d" % wb, xbcT2[wb][:].rearrange("p j t -> p (j t)"), xsp_d[gw], reads=[r_xsp[gw]], writes=r_xbc2[wb])
            k.dma("sp", "ldd%d" % wb, dtt2[wb][:].rearrange("p c h -> p (c h)"), dsp_d[gw], reads=[r_dsp[gw]], writes=[r_dtt2[wb]])

        def load_z(gw, cc):
            k.dma("sp", "ldz%d" % cc, szTc[cc][:], zsp_d[gw, cc], reads=[r_zsp[gw][cc]], writes=[r_szTc[cc]])

        def make_prepA(base, nw, cond, w, wb, getter, i0, gw, skip_front=False, next_front=None):
            pcs = l1_xbc_pieces(getter, [i0 + i for i in range(6)], 6, wb)

            def f0():
                if not skip_front:
                    l1_front(base, nw, w, cond, wb)
                pcs[0]()
                pcs[1]()

            def f1():
                if next_front is not None:
                    next_front("a0")
                pcs[2]()
                pcs[3]()

            def f2():
                pcs[4]()
                pcs[5]()
                l1_dt(wb)

            def f3():
                l1_z_part(getter, i0 + 6, wb, 0)
                l1_z_part(getter, i0 + 7, wb, 1)

            def f4():
                l1_z_part(getter, i0 + 8, wb, 2)
                if next_front is not None:
                    next_front("a1")

            def f5():
                l1_z_part(getter, i0 + 9, wb, 3)
                spill_window(gw, wb)
                if next_front is not None:
                    next_front("b")
            return [f0, f1, f2, f3, f4, f5]

        def win_srcs():
            return [wsrc(2048 + pc * 512) for pc in range(6)] + [wsrc(q * 512) for q in range(4)]

        par0 = 0
        preA_done = False
        getA_next = None
        for si, (base, nw, cond, has_init, oidx, cid0) in enumerate(seqs):
            nxt_seq = seqs[si + 1] if si + 1 < len(seqs) else None
            gw0 = cid0 // 2
            def wbA(w, par0=par0, nw=nw):
                return (par0 + nw - 1 - w) % 2

            first_w = nw - 1
            if preA_done:
                getA = getA_next
                posA = {w: 10 * (first_w - w) for w in range(nw)}
            else:
                getA = wstream([src for _w in range(nw) for src in win_srcs()])
                posA = {w: 10 * (first_w - w) for w in range(nw)}

            def prepA(w, base=base, nw=nw, cond=cond):
                nf = (lambda ph: l1_front(base, nw, w - 1, cond, wbA(w - 1), ph)) if w > 0 else None
                return make_prepA(base, nw, cond, w, wbA(w), getA, posA[w], gw0 + w,
                                  skip_front=(w != first_w), next_front=nf)

            if not preA_done:
                for fn in prepA(first_w):
                    fn()
            if has_init:
                state_load(1, stb_d)
            else:
                V("pool", lambda e: e.memset(S[1][:], 0.0), writes=[r_S[1]])
            if si == 0:
                load_wout(swout_d)
            for w in reversed(range(nw)):
                nxt = prepA(w - 1) if w > 0 else []
                chunkA(1, cid0 + 2 * w + 1, wbA(w), nxt[0:3])
                chunkA(0, cid0 + 2 * w, wbA(w), nxt[3:6])
            if oidx is not None:
                state_store(1, nsb_d[oidx])
            if has_init:
                state_load(0, stf_d)
            else:
                V("pool", lambda e: e.memset(S[0][:], 0.0), writes=[r_S[0]])

            def wbB(w, wbA=wbA):
                return (wbA(0) + w) % 2

            par0_next = 1 - wbB(nw - 1)
            if nxt_seq is not None:
                getA_next = wstream([src for _w in range(nxt_seq[1]) for src in win_srcs()])
            load_xd(gw0, wbB(0))
            load_z(gw0, 0)
            load_z(gw0, 1)
            for w in range(nw):
                wb = wbB(w)
                mid0 = {}
                mid1 = {}
                if w + 1 < nw:
                    mid0 = {1: (lambda w=w: load_xd(gw0 + w + 1, wbB(w + 1)))}
                    mid1 = {0: (lambda w=w: load_z(gw0 + w + 1, 0))}
                elif nxt_seq is not None:
                    nb, nnw, ncond, ncid0 = nxt_seq[0], nxt_seq[1], nxt_seq[2], nxt_seq[5]
                    nnf = None
                    if nnw > 1:
                        nnf = (lambda ph: l1_front(nb, nnw, nnw - 2, ncond, 1 - par0_next, ph))
                    nxa = make_prepA(nb, nnw, ncond, nnw - 1, par0_next, getA_next, 0, ncid0 // 2 + nnw - 1,
                                     next_front=nnf)
                    mid0 = {0: nxa[0], 2: nxa[1]}
                    mid1 = {0: nxa[2], 1: nxa[3], 2: nxa[4], 3: nxa[5]}
                chunkB(0, cid0 + 2 * w, base + w * 256, cond, wb, {}, mid0)
                chunkB(1, cid0 + 2 * w + 1, base + w * 256 + 128, cond, wb, {}, mid1)
                if w + 1 < nw:
                    load_z(gw0 + w + 1, 1)
            if oidx is not None:
                state_store(0, nsf_d[oidx])
            preA_done = nxt_seq is not None
            par0 = par0_next
        k.barrier()
        k.pop()

        k.finish(store_sems)
        print("instructions", k.nins, "waits", k.nwait)
    return nc


def _pack_inputs(inp):
    f = lambda a: np.ascontiguousarray(np.asarray(a, dtype=np.float32))

    def colmajor(v, nblk):
        return f(v).reshape(nblk, 128).T

    common = np.zeros((128, PK_N), np.float32)

    def put(name, arr):
        lo, hi = PK[name]
        common[:, lo:hi] = arr

    put("ada_b", np.concatenate([colmajor(inp["ada_b"][l], 24) for l in range(2)], axis=1))
    put("norm_g", np.concatenate([colmajor(inp["norm_g"][l], 8) for l in range(2)], axis=1))
    put("cb_in", colmajor(inp["conv_b_in"][0], 48))
    wdw = f(inp["conv_w_dw"][0])
    put("cw_dw", wdw.reshape(31, 16, 128).transpose(2, 1, 0).reshape(128, 16 * 31))
    put("cb_dw", colmajor(inp["conv_b_dw"][0], 16))
    put("cln_g", colmajor(inp["conv_ln_g"][0], 16))
    put("cln_b", colmajor(inp["conv_ln_b"][0], 16))
    wc = f(inp["ssd_w_conv"][0])
    put("sw_conv", wc.reshape(5, 24, 128).transpose(2, 1, 0).reshape(128, 24 * 5))
    put("sb_conv", colmajor(inp["ssd_b_conv"][0], 24))
    put("dtb", np.broadcast_to(np.concatenate([f(inp["ssd_dt_bias_f"][0]), f(inp["ssd_dt_bias_b"][0])])[None, :], (128, 64)))
    put("alog", np.broadcast_to(np.concatenate([f(inp["ssd_a_log_f"][0]), f(inp["ssd_a_log_b"][0])])[None, :], (128, 64)))
    put("sd", np.broadcast_to(f(inp["ssd_d"][0])[None, :], (128, 32)))
    put("ident", np.eye(128, dtype=np.float32))
    ada_bg = f(np.broadcast_to(np.concatenate([f(inp["ada_b"][l][2048:3072]) for l in range(2)])[None, :], (128, 2048)))
    sng = f(np.broadcast_to(f(inp["ssd_norm_g"][0])[None, :], (128, 2048)))
    fng = f(np.broadcast_to(f(inp["final_norm_g"])[None, :], (128, 1024)))
    shared = {
        "ada_bg": ada_bg, "sng": sng, "fng": fng,
        "ada_w": f(inp["ada_w"]), "cw_in": f(inp["conv_w_in"][0]), "cw_out": f(inp["conv_w_out"][0]),
        "sw_in": f(inp["ssd_w_in"][0]), "sw_out": f(inp["ssd_w_out"][0]),
    }
    xs = f(inp["x_sample"])
    xp = f(inp["x_prompt"])
    maps = []
    for i in range(NCORES):
        pk = common.copy()
        cond = np.stack([f(inp["c"][i]), f(inp["c_ctx"])], axis=1)
        lo, hi = PK["cond"]
        pk[:, lo:hi] = cond.reshape(8, 128, 2).transpose(1, 0, 2).reshape(128, 16)
        m = dict(shared)
        m["pk"] = pk
        m["x"] = np.concatenate([xs[i], xp[2 * i], xp[2 * i + 1]], axis=0)
        m["stf"] = f(inp["state_fwd"][i, 0]).reshape(2048, 128)
        m["stb"] = f(inp["state_bwd"][i, 0]).reshape(2048, 128)
        maps.append(m)
    return maps


_CACHE = {}


def kernel(**inputs):
    stop = bool(int(os.environ.get("K_STOP_L0", "0")))
    maps = _pack_inputs(inputs)
    nc = build_program(stop_after_l0=stop)
    ndbg = int(os.environ.get("K_NDBG", "0"))
    if ndbg:
        res = run_bass_kernel_spmd(nc, maps[:ndbg], core_ids=list(range(ndbg)))
        r = list(res.results) + [res.results[0]] * (NCORES - ndbg)
    else:
        res = run_bass_kernel_spmd(nc, maps, core_ids=list(range(NCORES)))
        r = res.results
    y_s = np.stack([r[i]["y"][:2048] for i in range(NCORES)], axis=0)
    y_p = np.concatenate([r[i]["y"][2048:].reshape(2, 256, D) for i in range(NCORES)], axis=0)
    nsf = np.concatenate([r[i]["nsf"].reshape(2, 1, 32, 64, 128) for i in range(NCORES)], axis=0)
    nsb = np.concatenate([r[i]["nsb"].reshape(2, 1, 32, 64, 128) for i in range(NCORES)], axis=0)
    return (y_p.astype(np.float32), y_s.astype(np.float32), nsf.astype(np.float32), nsb.astype(np.float32))
```
